# Optimizing a Trainium2 kernel written in Bass

```python
import jax, jax.numpy as jnp
from jax import lax
import numpy as np

D_MODEL = 2048
BATCH = 1
SEQ = 16384
DEPTH = 1

HEAD_DIM = 128
HEADS_PER_GROUP = 8
DILATION_GROUPS = ((128, 1), (512, 4), (2048, 16))
N_GROUPS = len(DILATION_GROUPS)
N_QKV_HEADS = N_GROUPS * HEADS_PER_GROUP
ATTN_QKV_WIDTH = N_QKV_HEADS * HEAD_DIM
ATTN_OUT_WIDTH = HEADS_PER_GROUP * HEAD_DIM
ROT_DIM = HEAD_DIM // 4
ROPE_THETA = 500000.0
CONV_WIDTH = D_MODEL
CONV_K = 3
EPS = 1e-6
SPLIT_SIZES = (ATTN_QKV_WIDTH, ATTN_QKV_WIDTH, ATTN_QKV_WIDTH, ATTN_OUT_WIDTH,
               CONV_WIDTH, CONV_WIDTH, CONV_WIDTH, CONV_WIDTH,
               D_MODEL, D_MODEL)
PROJ_WIDTH = sum(SPLIT_SIZES)
SPLIT_POINTS = tuple(int(s) for s in np.cumsum(SPLIT_SIZES)[:-1])
NEG_INF = -1e30

kernel_name = "hybrid_dilated_attn_shortconv_gated_merge"


def rms_norm(x, g):
    xf = x.astype(jnp.float32)
    y = xf * lax.rsqrt(jnp.mean(xf * xf, axis=-1, keepdims=True) + EPS)
    return (y * g.astype(jnp.float32)).astype(x.dtype)


def rope_tables(positions, dtype):
    inv_freq = ROPE_THETA ** (-jnp.arange(0, ROT_DIM, 2, dtype=jnp.float32) / ROT_DIM)
    ang = positions.astype(jnp.float32)[..., None] * inv_freq
    return jnp.cos(ang)[:, :, None, :].astype(dtype), jnp.sin(ang)[:, :, None, :].astype(dtype)


def apply_partial_rope(t, cos, sin):
    half = ROT_DIM // 2
    t1, t2, rest = t[..., :half], t[..., half:ROT_DIM], t[..., ROT_DIM:]
    return jnp.concatenate([t1 * cos - t2 * sin, t1 * sin + t2 * cos, rest], axis=-1)


def dilated_window_attention(q, k, v, dilation, w_sub):
    B, S, H, Dh = q.shape
    span = dilation * w_sub
    Sp = ((S + span - 1) // span) * span
    L = Sp // dilation
    nb = L // w_sub

    def to_sub(t):
        t = jnp.pad(t, ((0, 0), (0, Sp - S), (0, 0), (0, 0)))
        t = t.reshape(B, L, dilation, H, Dh).transpose(0, 3, 2, 1, 4)
        return t.reshape(B, H, dilation, nb, w_sub, Dh)

    def with_prev(t):
        prev = jnp.pad(t[:, :, :, :-1], ((0, 0), (0, 0), (0, 0), (1, 0), (0, 0), (0, 0)))
        return jnp.concatenate([prev, t], axis=4)

    qs = to_sub(q)
    kk = with_prev(to_sub(k))
    vv = with_prev(to_sub(v))
    scores = jnp.einsum('bhrnqd,bhrnkd->bhrnqk', qs, kk).astype(jnp.float32) * (Dh ** -0.5)
    qi = jnp.arange(w_sub)[:, None]
    ki = jnp.arange(2 * w_sub)[None, :]
    dist = qi + w_sub - ki
    band = (dist >= 0) & (dist <= w_sub)
    first = (jnp.arange(nb)[:, None, None] > 0) | (ki[None] >= w_sub)
    mask = band[None] & first
    scores = jnp.where(mask, scores, NEG_INF)
    lse = jax.nn.logsumexp(scores, axis=-1)
    p = jnp.exp(scores - lse[..., None])
    out = jnp.einsum('bhrnqk,bhrnkd->bhrnqd', p, vv.astype(jnp.float32))
    out = out.reshape(B, H, dilation, L, Dh).transpose(0, 3, 2, 1, 4).reshape(B, Sp, H, Dh)[:, :S]
    lse = lse.reshape(B, H, dilation, L).transpose(0, 3, 2, 1).reshape(B, Sp, H)[:, :S]
    return out, lse


def causal_depthwise_conv(u, w):
    return lax.conv_general_dilated(
        u, w[:, None, :].astype(u.dtype), window_strides=(1,), padding=[(CONV_K - 1, 0)],
        dimension_numbers=('NWC', 'WIO', 'NWC'), feature_group_count=u.shape[-1])


def setup_inputs(seed: int = 0) -> dict:
    key = jax.random.key(seed)
    ks = jax.random.split(key, 13)
    f32 = jnp.float32
    x = jax.random.normal(ks[0], (BATCH, SEQ, D_MODEL), f32)
    c = jax.random.normal(ks[1], (BATCH, D_MODEL), f32)
    offset = jax.random.randint(ks[2], (BATCH, 1), 0, 1024, dtype=jnp.int32)
    positions = (jnp.arange(SEQ, dtype=jnp.int32)[None, :] + offset).astype(jnp.int32)
    g_pre = 1.0 + 0.05 * jax.random.normal(ks[3], (DEPTH, D_MODEL), f32)
    w_ada = 0.5 * D_MODEL ** -0.5 * jax.random.normal(ks[4], (DEPTH, D_MODEL, 3 * D_MODEL), f32)
    b_ada = 0.02 * jax.random.normal(ks[5], (DEPTH, 3 * D_MODEL), f32)
    w_in = D_MODEL ** -0.5 * jax.random.normal(ks[6], (DEPTH, D_MODEL, PROJ_WIDTH), f32)
    conv_w = CONV_K ** -0.5 * jax.random.normal(ks[7], (DEPTH, CONV_K, CONV_WIDTH), f32)
    w_attn_o = ATTN_OUT_WIDTH ** -0.5 * jax.random.normal(ks[8], (DEPTH, ATTN_OUT_WIDTH, D_MODEL), f32)
    w_conv_o = CONV_WIDTH ** -0.5 * jax.random.normal(ks[9], (DEPTH, CONV_WIDTH, D_MODEL), f32)
    w_o = D_MODEL ** -0.5 * jax.random.normal(ks[10], (DEPTH, D_MODEL, D_MODEL), f32)
    g_post = 1.0 + 0.05 * jax.random.normal(ks[11], (DEPTH, D_MODEL), f32)
    return {"x": x, "c": c, "positions": positions, "g_pre": g_pre, "w_ada": w_ada,
            "b_ada": b_ada, "w_in": w_in, "conv_w": conv_w, "w_attn_o": w_attn_o,
            "w_conv_o": w_conv_o, "w_o": w_o, "g_post": g_post}


def reference(x, c, positions, g_pre, w_ada, b_ada, w_in, conv_w, w_attn_o, w_conv_o, w_o, g_post):
    B, S, _ = x.shape
    cos, sin = rope_tables(positions, x.dtype)
    for l in range(DEPTH):
        mod = (c @ w_ada[l] + b_ada[l])[:, None, :]
        shift, scale, gate = jnp.split(mod, 3, axis=-1)
        h = rms_norm(x, g_pre[l]) * (1.0 + scale) + shift

        proj = h @ w_in[l]
        q, k, v, z_attn, cb, cc, cx, z_conv, g_attn, g_conv = jnp.split(proj, SPLIT_POINTS, axis=-1)

        q = apply_partial_rope(q.reshape(B, S, N_QKV_HEADS, HEAD_DIM), cos, sin)
        k = apply_partial_rope(k.reshape(B, S, N_QKV_HEADS, HEAD_DIM), cos, sin)
        q = q.reshape(B, S, N_GROUPS, HEADS_PER_GROUP, HEAD_DIM)
        k = k.reshape(B, S, N_GROUPS, HEADS_PER_GROUP, HEAD_DIM)
        v = v.reshape(B, S, N_GROUPS, HEADS_PER_GROUP, HEAD_DIM)
        outs, lses = [], []
        for g, (window, dilation) in enumerate(DILATION_GROUPS):
            o_g, lse_g = dilated_window_attention(q[:, :, g], k[:, :, g], v[:, :, g],
                                                  dilation, window // dilation)
            outs.append(o_g)
            lses.append(lse_g)
        alpha = jax.nn.softmax(jnp.stack(lses, axis=0), axis=0)
        o = jnp.sum(alpha[..., None] * jnp.stack(outs, axis=0), axis=0)
        o = o.astype(x.dtype).reshape(B, S, ATTN_OUT_WIDTH)
        y_attn = (o * jax.nn.silu(z_attn)) @ w_attn_o[l]

        u = causal_depthwise_conv(cc * cx, conv_w[l])
        y_conv = (cb * u * jax.nn.silu(z_conv)) @ w_conv_o[l]

        merged = jax.nn.sigmoid(g_attn) * y_attn + jax.nn.sigmoid(g_conv) * y_conv
        y = merged @ w_o[l]
        x = x + gate * rms_norm(y, g_post[l])
    return x
```

```python
from contextlib import ExitStack
import numpy as np
import concourse.bass as bass
import concourse.mybir as mybir
from concourse.bass_utils import run_bass_kernel_spmd

F32 = mybir.dt.float32
BF16 = mybir.dt.bfloat16
I32 = mybir.dt.int32
U8 = mybir.dt.uint8
AF = mybir.ActivationFunctionType
ALU = mybir.AluOpType
ESZ = {F32: 4, BF16: 2, I32: 4, U8: 1}

NCORES = 8
D = 2048
SEQ = 16384
TOK = SEQ // NCORES
CH = 1024
HALO = 2048
KC = D // 128
PROJ = 22528
EPS = 1e-6
CELL = 256
ENGS = ("pe", "act", "dve", "pool", "sp")
SBUF_BYTES = 207 * 1024


class View:
    __slots__ = ("ap", "space", "lo", "hi")

    def __init__(self, ap, space, lo, hi):
        self.ap, self.space, self.lo, self.hi = ap, space, lo, hi


class Buf:
    def __init__(self, S, lo, shape, dtype, parts=128):
        self.S, self.lo, self.shape, self.dtype, self.parts = S, lo, list(shape), dtype, parts
        self.esz = ESZ[dtype]
        self.n = int(np.prod(shape))
        ap = S.arena[0:parts, lo:lo + self.n * self.esz]
        if dtype != U8:
            ap = ap.bitcast(dtype)
        if len(shape) == 2:
            ap = ap.rearrange("p (a b) -> p a b", a=shape[0])
        elif len(shape) == 3:
            ap = ap.rearrange("p (a b c) -> p a b c", a=shape[0], b=shape[1])
        self.ap = ap
        self.hi = lo + self.n * self.esz

    def all(self):
        return View(self.ap, "sb", self.lo, self.hi)

    def v(self, *idx, p=None):
        strides = []
        s = 1
        for d in reversed(self.shape):
            strides.insert(0, s)
            s *= d
        key = [slice(None) if p is None else slice(p[0], p[1])]
        lo = 0
        hi = 0
        for i, d in enumerate(self.shape):
            ix = idx[i] if i < len(idx) else None
            if ix is None:
                key.append(slice(None))
                a, b = 0, d
            elif isinstance(ix, int):
                key.append(ix)
                a, b = ix, ix + 1
            else:
                a, b = ix[0], ix[1]
                st = ix[2] if len(ix) > 2 else 1
                key.append(slice(a, b, st) if st != 1 else slice(a, b))
            assert 0 <= a < b <= d, (idx, self.shape)
            lo += a * strides[i]
            hi += (b - 1) * strides[i]
        hi += 1
        return View(self.ap[tuple(key)], "sb", self.lo + lo * self.esz, self.lo + hi * self.esz)


class Sched:
    def __init__(self, nc, stack, sbuf_bytes, n_dma_sems=12):
        self.nc = nc
        self.ops = {e: [] for e in ENGS}
        self.sems = {}
        for e in ENGS:
            self.sems[e] = stack.enter_context(nc.semaphore("s_" + e))
        self.cnt = {e: 0 for e in ENGS}
        self.pending = {e: False for e in ENGS}
        self.known = {e: {} for e in ENGS}
        self.ndma = n_dma_sems
        self.dma_val = {}
        self.dma_rr = {}
        for q in ("sp", "pool", "act"):
            for i in range(n_dma_sems):
                self.sems[("d", q, i)] = stack.enter_context(nc.semaphore("s_dma_%s%d" % (q, i)))
                self.dma_val[(q, i)] = 0
            self.dma_rr[q] = 0
        ncell = (sbuf_bytes + CELL - 1) // CELL
        self.cw = {"sb": [None] * ncell, "ps": [None] * 8}
        self.cr = {"sb": [dict() for _ in range(ncell)], "ps": [dict() for _ in range(8)]}
        self.arena = stack.enter_context(nc.sbuf_tensor("arena", [128, sbuf_bytes], U8))
        self.psum = [stack.enter_context(nc.psum_tensor("pb%d" % i, [128, 512], F32)) for i in range(8)]
        self.sbuf_bytes = sbuf_bytes
        self.rr = {}

    def buf(self, lo, shape, dtype, parts=128):
        b = Buf(self, lo, shape, dtype, parts)
        assert b.hi <= self.sbuf_bytes, "arena overflow"
        return b

    def ps(self, bank, dtype=F32, parts=128, cols=None):
        t = self.psum[bank]
        ap = t[0:parts, :]
        if dtype != F32:
            ap = ap.bitcast(dtype)
        if cols is not None:
            ap = ap[:, cols[0]:cols[1]]
        return View(ap, "ps", bank, bank + 1)

    def bank(self, pool):
        i = self.rr.get(pool, 0)
        self.rr[pool] = (i + 1) % len(pool)
        return pool[i]

    def _cells(self, v):
        if v.space == "ps":
            return range(v.lo, v.hi)
        return range(v.lo // CELL, (v.hi + CELL - 1) // CELL)

    def _deps(self, eng, reads, writes):
        need = {}

        def add(tok):
            if tok is None:
                return
            k, val = tok
            if k == "pe" and eng == "pe":
                return
            if need.get(k, 0) < val:
                need[k] = val

        for v in reads:
            cw, cr = self.cw[v.space], self.cr[v.space]
            for c in self._cells(v):
                add(cw[c])
                if v.space == "ps":
                    for k, val in cr[c].items():
                        if k != eng:
                            add((k, val))
        for v in writes:
            cw, cr = self.cw[v.space], self.cr[v.space]
            for c in self._cells(v):
                add(cw[c])
                for k, val in cr[c].items():
                    add((k, val))
        kn = self.known[eng]
        waits = []
        for k, val in need.items():
            if kn.get(k, 0) >= val:
                continue
            kn[k] = val
            waits.append((k, val))
        return waits

    def _record(self, tok, reads, writes):
        k, val = tok
        for v in reads:
            cr = self.cr[v.space]
            for c in self._cells(v):
                cr[c][k] = val
        for v in writes:
            cw, cr = self.cw[v.space], self.cr[v.space]
            for c in self._cells(v):
                cw[c] = tok
                cr[c] = {}

    def op(self, eng, fn, reads=(), writes=(), signal=True):
        waits = self._deps(eng, reads, writes)
        tok = (eng, self.cnt[eng] + 1)
        if signal:
            self.cnt[eng] += 1
            self.pending[eng] = False
        else:
            self.pending[eng] = True
        self._record(tok, reads, writes)
        self.ops[eng].append((waits, fn, ("inc", eng) if signal else None))
        return tok

    def dma(self, queue, out, in_, reads=(), writes=(), **kw):
        i = self.dma_rr[queue]
        self.dma_rr[queue] = (i + 1) % self.ndma
        waits = self._deps(queue, reads, writes)
        kn = self.known[queue]
        sk = ("d", queue, i)
        if self.dma_val[(queue, i)] > 0 and kn.get(sk, 0) < self.dma_val[(queue, i)]:
            kn[sk] = self.dma_val[(queue, i)]
            waits.append((sk, self.dma_val[(queue, i)]))
        self.dma_val[(queue, i)] += 16
        tok = (sk, self.dma_val[(queue, i)])
        self._record(tok, reads, writes)

        def fn(e, out=out, in_=in_, kw=kw):
            return e.dma_start(out=out, in_=in_, **kw)

        self.ops[queue].append((waits, fn, ("dma", sk)))
        return tok

    def wait_tokens(self, eng, toks):
        kn = self.known[eng]
        waits = []
        for k, val in toks:
            if kn.get(k, 0) < val:
                kn[k] = val
                waits.append((k, val))
        if waits:
            self.ops[eng].append((waits, None, None))

    def replay(self, block):
        sems = self.sems

        def run(e, lst):
            for waits, fn, post in lst:
                for k, val in waits:
                    e.wait_ge(sems[k], val)
                if fn is None:
                    continue
                ins = fn(e)
                if post is None:
                    continue
                if post[0] == "inc":
                    ins.then_inc(sems[post[1]], 1)
                else:
                    ins.then_inc(sems[post[1]], 16)

        for e in ENGS:
            assert not self.pending[e], "engine %s ends with unsignaled instruction" % e

        @block.tensor
        def _(e):
            run(e, self.ops["pe"])

        @block.scalar
        def _(e):
            run(e, self.ops["act"])

        @block.vector
        def _(e):
            run(e, self.ops["dve"])

        @block.gpsimd
        def _(e):
            run(e, self.ops["pool"])

        @block.sync
        def _(e):
            run(e, self.ops["sp"])


CF_IDENT = 0
CF_GPRE = 128
CF_GPOST = 144
CF_BADA = 160
CF_C = 208
CF_CONVW = 224
CF_INVF = 272
CF_SGN = 273
CF_ONE = 274
CF_HV = 275
CF_EPS = 276
NCF = 277
CB_IDENT = 0
CB_ONES = 128
CB_PM = 256
CB_LU4 = 288
CB_LU4H1 = 800
CB_LU4H2 = 1312
CB_G2A = 1824
CB_G2B = 2336
NCB = 2848

GEN = (0, 1, 2, 3)
OB = (4, 5)
DB = (6, 7)
ALLB = (0, 1, 2, 3, 4, 5, 6, 7)
ALLB7 = (0, 1, 2, 3, 4, 5, 6)


def build_program():
    nc = bass.Bass("TRN2", target_bir_lowering=False)
    xin = nc.dram_tensor("xin", [HALO + TOK, D], F32, kind="ExternalInput").ap()
    pos_d = nc.dram_tensor("pos32", [32, HALO + TOK], I32, kind="ExternalInput").ap()
    cf_d = nc.dram_tensor("cstf", [128, NCF], F32, kind="ExternalInput").ap()
    cb_d = nc.dram_tensor("cstb", [128, NCB], F32, kind="ExternalInput").ap()
    wada_d = nc.dram_tensor("wada", [KC, 128, 3 * D], F32, kind="ExternalInput").ap()
    win_d = nc.dram_tensor("win", [PROJ // 128, 128, KC * 128], F32, kind="ExternalInput").ap()
    wao_d = nc.dram_tensor("wao", [16, 128, 8 * 128], F32, kind="ExternalInput").ap()
    wco_d = nc.dram_tensor("wco", [16, 128, KC * 128], F32, kind="ExternalInput").ap()
    wo_d = nc.dram_tensor("wo", [16, 128, KC * 128], F32, kind="ExternalInput").ap()
    out_d = nc.dram_tensor("out", [TOK, D], F32, kind="ExternalOutput").ap()

    with ExitStack() as st:
        S = Sched(nc, st, SBUF_BYTES)
        top = [0]

        def alloc(nbytes):
            lo = (top[0] + CELL - 1) // CELL * CELL
            top[0] = lo + nbytes
            assert top[0] <= SBUF_BYTES, "SBUF overflow %d" % top[0]
            return lo

        def newbuf(shape, dtype, parts=128):
            return S.buf(alloc(int(np.prod(shape)) * ESZ[dtype]), shape, dtype, parts)

        cf = newbuf([NCF], F32)
        cb = newbuf([NCB], BF16)
        modv = newbuf([64], F32)
        modraw = newbuf([64], F32)
        NW = 3
        wbufs = [newbuf([KC, 128], BF16) for _ in range(NW)]
        hT_own = newbuf([KC, CH], BF16)
        aT = newbuf([8, CH], BF16)
        hT_h2 = newbuf([KC, 2], BF16)
        X0 = alloc(0)
        XBYTES = SBUF_BYTES - X0
        assert XBYTES >= 130 * 1024, XBYTES

        def xbuf(off, shape, dtype, parts=128):
            b = S.buf(X0 + off, shape, dtype, parts)
            assert b.hi <= SBUF_BYTES
            return b

        hT_halo = xbuf(0, [KC, HALO], BF16)
        A0 = 64 * 1024
        cosT = xbuf(A0, [CH + HALO], F32, parts=32)
        sinT = xbuf(A0 + 12288, [CH + HALO], F32, parts=32)
        A1 = A0 + 24576
        qT = xbuf(A1, [CH], BF16)
        kT = xbuf(A1 + 2048, [CH + HALO], BF16)
        vT = xbuf(A1 + 8192, [CH + HALO], BF16)
        Vt = xbuf(A1 + 14336, [32, 128], BF16)
        szb = xbuf(A1 + 22528, [CH], F32)
        numb = xbuf(A1 + 26624, [CH], F32)
        denb = xbuf(A1 + 30720, [CH], F32)
        pTb = [xbuf(A1 + 34816 + i * 1024, [512], BF16) for i in range(4)]
        rtmp = [xbuf(A1 + 38912 + i * 2048, [512], F32, parts=32) for i in range(2)]
        A_END = A1 + 43008
        assert A_END <= XBYTES, (A_END, XBYTES)
        xts = [xbuf(A1, [D], F32), S.buf(aT.lo, [D], F32), S.buf(aT.lo + 8192, [D], F32)]
        xh8 = [xbuf(A1 + 8192 + i * 4096, [D], BF16) for i in range(8)]
        stat8 = [xbuf(A1 + 40960 + i * 256, [8], F32) for i in range(8)]
        assert A1 + 40960 + 8 * 256 <= A_END
        rp_k = xbuf(A_END, [1536], F32, parts=32)
        assert A_END + 6144 <= XBYTES
        wa = [xbuf(A0 + i * 2048, [512], F32) for i in range(8)] + [xbuf(A_END + i * 2048, [512], F32) for i in range(4)]
        acc = xbuf(A0 + 16384, [2048], F32)
        assert A_END + 4 * 2048 <= XBYTES
        wa_g = [xbuf(98304 + i * 2048, [512], F32) for i in range(4)]
        acc_g = xbuf(98304 + 8192, [2048], F32)
        gcT = xbuf(0, [KC, CH], BF16)
        mT = xbuf(32768, [KC, CH], BF16)
        yT = xbuf(65536, [KC, 512], F32)
        yTb = [yT, xbuf(0, [KC, 512], F32)]
        outb = xbuf(98304, [4, D], F32)
        ccs = [xbuf(65536 + i * 2048, [512], F32) for i in range(2)]
        pbuf = xbuf(65536 + 4096, [CH + 2], F32)
        ubuf = xbuf(65536 + 4096 + 4352, [CH], F32)
        szc = [xbuf(65536 + 12544 + i * 2048, [512], F32) for i in range(2)]
        cch = xbuf(65536 + 16640, [2], F32)
        sg = [xbuf(65536 + 16896 + i * 2048, [512], F32) for i in range(4)]
        rstd = xbuf(98304 + 32768, [512], F32)
        o0 = 98304 + 32768 + 2048
        sqb = [xbuf(o0 + i * 1024, [512], BF16) for i in range(2)]
        otmp = [xbuf(o0 + 2048 + i * 2048, [512], F32) for i in range(2)]
        rstdb = [rstd, xbuf(o0 + 6144, [512], F32)]
        assert o0 + 8192 <= XBYTES, (o0, XBYTES)
        assert o0 + 2048 + 4096 <= XBYTES, (o0, XBYTES)

        identF = cf.v((CF_IDENT, CF_IDENT + 128))
        identB = cb.v((CB_IDENT, CB_IDENT + 128))
        onesB = cb.v((CB_ONES, CB_ONES + 128))

        def cfcol(c0, n=1, p=None):
            return cf.v((c0, c0 + n), p=p)

        S.dma("sp", cf.ap, cf_d, writes=[cf.all()])
        S.dma("pool", cb.ap[:, 0:1424], cb_d[:, 0:1424], writes=[cb.v((0, 1424))])
        S.dma("pool", cb.ap[:, 1424:NCB], cb_d[:, 1424:NCB], writes=[cb.v((1424, NCB))])

        mod_steps = []
        gate_steps = []

        def mk_third(j, wabufs, accb, queue, steps):
            cnt = [0]
            for kc in range(KC):
                for pc in range(4):
                    def step(kc=kc, pc=pc):
                        w = wabufs[cnt[0] % len(wabufs)]
                        cnt[0] += 1
                        col = j * 2048 + pc * 512
                        S.dma(queue, w.ap, wada_d[kc][:, col:col + 512], writes=[w.all()])
                        av = accb.v((pc * 512, pc * 512 + 512))
                        if kc == 0:
                            S.op("dve", lambda e: e.tensor_scalar(out=av.ap, in0=w.ap, scalar1=cf.ap[:, CF_C:CF_C + 1],
                                                                  scalar2=None, op0=ALU.mult),
                                 reads=[w.all(), cf.all()], writes=[av])
                        else:
                            S.op("dve", lambda e: e.scalar_tensor_tensor(
                                out=av.ap, in0=w.ap, scalar=cf.ap[:, CF_C + kc:CF_C + kc + 1], in1=av.ap,
                                op0=ALU.mult, op1=ALU.add), reads=[w.all(), av, cf.all()], writes=[av])
                    steps.append(step)

            def fin_third():
                pm = S.ps(S.bank(ALLB))
                for i in range(16):
                    S.op("pe", lambda e, i=i: e.matmul(pm.ap[:, i:i + 1], lhsT=accb.ap[:, 128 * i:128 * i + 128],
                                                       rhs=cf.ap[:, CF_ONE:CF_ONE + 1], start=True, stop=True),
                         reads=[accb.all(), cf.all()], writes=[pm], signal=(i == 15))
                b0 = CF_BADA + 16 * j
                mr = modraw.v((16 * j, 16 * j + 16))
                S.op("dve", lambda e: e.tensor_tensor(out=mr.ap, in0=pm.ap[:, 0:16], in1=cf.ap[:, b0:b0 + 16], op=ALU.add),
                     reads=[pm, cf.all()], writes=[modraw.all()])
            steps.append(fin_third)

        mk_third(0, wa, acc, "pool", mod_steps)
        mk_third(1, wa, acc, "pool", mod_steps)

        def fin_gs():
            S.op("dve", lambda e: e.scalar_tensor_tensor(out=modv.ap[:, 0:16], in0=modraw.ap[:, 16:32], scalar=1.0,
                                                         in1=cf.ap[:, CF_GPRE:CF_GPRE + 16], op0=ALU.add, op1=ALU.mult),
                 reads=[modraw.all(), cf.all()], writes=[modv.all()])
            S.op("dve", lambda e: e.tensor_copy(out=modv.ap[:, 16:32], in_=modraw.ap[:, 0:16]),
                 reads=[modraw.all()], writes=[modv.all()])
        mod_steps.append(fin_gs)

        mk_third(2, wa_g, acc_g, "sp", gate_steps)

        def fin_gg():
            S.op("dve", lambda e: e.tensor_tensor(out=modv.ap[:, 32:48], in0=modraw.ap[:, 32:48],
                                                  in1=cf.ap[:, CF_GPOST:CF_GPOST + 16], op=ALU.mult),
                 reads=[modraw.all(), cf.all()], writes=[modv.all()])
        gate_steps.append(fin_gg)

        def gate_slot(n):
            for _ in range(n):
                if gate_steps:
                    gate_steps.pop(0)()

        def mod_slot(n=4):
            for _ in range(n):
                if mod_steps:
                    mod_steps.pop(0)()

        wseq = []
        for run in range(2):
            for h in range(8):
                for g in range(3):
                    wseq += [("in", g * 8 + h), ("in", 24 + g * 8 + h), ("in", 48 + g * 8 + h)]
                wseq.append(("in", 72 + h))
            for c in range(16):
                wseq += [("in", 96 + c), ("in", 112 + c), ("in", 80 + c), ("in", 128 + c)]
            for f in range(16):
                wseq += [("ao", f), ("in", 144 + f), ("co", f), ("in", 160 + f)]
            for T in range(2):
                for f in range(16):
                    wseq.append(("o", f))
        wstate = {"issued": 0, "next": 0}

        def w_issue(j):
            kind, idx = wseq[j]
            b = wbufs[j % NW]
            if kind == "ao":
                S.dma("pool", b.ap[:, 0:8, :], wao_d[idx].rearrange("p (a b) -> p a b", a=8), writes=[b.all()])
            else:
                src = {"in": win_d, "co": wco_d, "o": wo_d}[kind][idx]
                S.dma("pool", b.ap, src.rearrange("p (a b) -> p a b", a=KC), writes=[b.all()])

        def w_get(expect):
            i = wstate["next"]
            assert wseq[i] == expect, (i, wseq[i], expect)
            wstate["next"] = i + 1
            while wstate["issued"] < min(len(wseq), i + NW):
                w_issue(wstate["issued"])
                wstate["issued"] += 1
            return wbufs[i % NW]

        deferred = []

        def flush_deferred():
            lst = deferred[:]
            del deferred[:]
            for fn in lst:
                fn()

        def hT_src(c0, c1):
            if c1 <= HALO:
                return lambda kc: hT_halo.v(kc, (c0, c1))
            assert c0 >= HALO
            return lambda kc: hT_own.v(kc, (c0 - HALO, c1 - HALO))

        def proj(wb, c0, c1, nk=KC, src=None, pool=GEN):
            n = c1 - c0
            bank = S.bank(pool)
            pv = S.ps(bank, cols=(0, n))
            srcf = src if src is not None else hT_src(c0, c1)
            for kc in range(nk):
                r = srcf(kc)
                S.op("pe", lambda e, kc=kc, r=r: e.matmul(pv.ap, lhsT=wb.ap[:, kc, :], rhs=r.ap,
                                                          start=(kc == 0), stop=(kc == nk - 1)),
                     reads=[wb.v(kc), r], writes=[pv], signal=(kc == nk - 1))
            flush_deferred()
            return pv

        def tiles(c0, c1):
            res = []
            c = c0
            while c < c1:
                lim = HALO if c < HALO else c1
                e_ = min(c + 512, lim, c1)
                res.append((c, e_))
                c = e_
            return res

        out_toks = []

        def emit_run(run):
            r0 = run * CH
            NT = CH + HALO

            def rope_tables():
                inv2pi = float(np.float32(1.0 / (2.0 * np.pi)))
                MAGIC = 12582912.0
                C1 = 6.28125
                C2 = float(np.float32(2.0 * np.pi - 6.28125))
                PI_LO = 3.1415925
                def half(hf):
                    a, b = hf * 1536, hf * 1536 + 1536
                    sv = sinT.v((a, b))
                    cv = cosT.v((a, b))
                    kv_ = rp_k.all()
                    S.dma("pool", sv.ap, pos_d[:, r0 + a:r0 + b], writes=[sv])
                    S.op("dve", lambda e: e.tensor_scalar(out=sv.ap, in0=sv.ap, scalar1=cf.ap[0:32, CF_INVF:CF_INVF + 1],
                                                           scalar2=None, op0=ALU.mult), reads=[sv, cf.all()], writes=[sv])
                    S.op("dve", lambda e: e.tensor_scalar(out=kv_.ap, in0=sv.ap, scalar1=inv2pi, scalar2=MAGIC,
                                                           op0=ALU.mult, op1=ALU.add), reads=[sv], writes=[kv_])
                    S.op("dve", lambda e: e.tensor_scalar(out=kv_.ap, in0=kv_.ap, scalar1=MAGIC, scalar2=None,
                                                           op0=ALU.subtract), reads=[kv_], writes=[kv_])
                    for Cc in (C1, C2):
                        S.op("dve", lambda e, Cc=Cc: e.tensor_scalar(out=cv.ap, in0=kv_.ap, scalar1=-Cc, scalar2=None, op0=ALU.mult),
                             reads=[kv_], writes=[cv])
                        S.op("dve", lambda e: e.tensor_tensor(out=sv.ap, in0=sv.ap, in1=cv.ap, op=ALU.add),
                             reads=[sv, cv], writes=[sv])
                    S.op("dve", lambda e: e.tensor_scalar(out=sv.ap, in0=sv.ap, scalar1=PI_LO, scalar2=-PI_LO,
                                                          op0=ALU.min, op1=ALU.max), reads=[sv], writes=[sv])
                half(0)
                half(1)

            def rope_tables_act():
                def half(hf):
                    a, b = hf * 1536, hf * 1536 + 1536
                    sv = sinT.v((a, b))
                    cv = cosT.v((a, b))
                    S.op("act", lambda e: e.activation(out=cv.ap, in_=sv.ap, func=AF.Sin, scale=0.5), reads=[sv], writes=[cv])
                    S.op("act", lambda e: e.activation(out=sv.ap, in_=sv.ap, func=AF.Sin, scale=cf.ap[0:32, CF_SGN:CF_SGN + 1]),
                         reads=[sv, cf.all()], writes=[sv])
                    S.op("dve", lambda e: e.tensor_tensor(out=cv.ap, in0=cv.ap, in1=cv.ap, op=ALU.mult), reads=[cv], writes=[cv])
                    S.op("dve", lambda e: e.tensor_scalar(out=cv.ap, in0=cv.ap, scalar1=-2.0, scalar2=1.0,
                                                           op0=ALU.mult, op1=ALU.add), reads=[cv], writes=[cv])
                half(0)
                half(1)


            tile_ctr = [0]

            def stage1(G):
                def fin(i, x_, st):
                    S.op("act", lambda e: e.activation(out=st.ap[:, 2:3], in_=st.ap[:, 0:1], func=AF.Sqrt, scale=1.0 / D, bias=cf.ap[:, CF_EPS:CF_EPS + 1]),
                         reads=[st.all(), cf.all()], writes=[st.all()])
                    S.op("dve", lambda e: e.reciprocal(out=st.ap[:, 3:4], in_=st.ap[:, 2:3]),
                         reads=[st.all()], writes=[st.all()])
                    xh_ = xh8[(G % 2) * 4 + i]
                    S.op("dve", lambda e: e.tensor_scalar(out=xh_.ap, in0=x_.ap, scalar1=st.ap[:, 3:4],
                                                          scalar2=None, op0=ALU.mult),
                         reads=[x_.all(), st.all()], writes=[xh_.all()])
                prevt = None
                for i in range(4):
                    row = r0 + G * 512 + i * 128
                    x_ = xts[tile_ctr[0] % 3]
                    tile_ctr[0] += 1
                    st = stat8[(G % 2) * 4 + i]
                    xh_ = xh8[(G % 2) * 4 + i]
                    S.dma("sp", x_.ap, xin[row:row + 128, :], writes=[x_.all()])
                    S.op("act", lambda e, x_=x_, st=st, xh_=xh_: e.activation(out=xh_.ap, in_=x_.ap, func=AF.Square,
                                                                           accum_out=st.ap[:, 0:1]),
                         reads=[x_.all()], writes=[xh_.all(), st.all()])
                    if prevt is not None:
                        fin(*prevt)
                    prevt = (i, x_, st)
                    if run == 0:
                        mod_slot()
                fin(*prevt)

            def stage2(G):
                for kc in range(KC):
                    bank = S.bank(ALLB)
                    pb = S.ps(bank, BF16, cols=(0, 512))
                    for i in range(4):
                        xh_ = xh8[(G % 2) * 4 + i]
                        S.op("pe", lambda e, i=i, kc=kc, pb=pb, xh_=xh_: e.transpose(out=pb.ap[:, i * 128:(i + 1) * 128],
                                                                                   in_=xh_.ap[:, kc * 128:(kc + 1) * 128],
                                                                                   identity=identB.ap),
                             reads=[xh_.v((kc * 128, kc * 128 + 128)), identB], writes=[pb], signal=(i == 3))
                    c0 = G * 512
                    dst = hT_halo.v(kc, (c0, c0 + 512)) if c0 < HALO else hT_own.v(kc, (c0 - HALO, c0 - HALO + 512))
                    if run == 0:
                        if kc % 2 == 0:
                            S.op("act", lambda e, pb=pb, dst=dst: e.copy(out=dst.ap, in_=pb.ap), reads=[pb], writes=[dst])
                        else:
                            S.op("dve", lambda e, pb=pb, dst=dst: e.tensor_copy(out=dst.ap, in_=pb.ap), reads=[pb], writes=[dst])
                    else:
                        if kc % 2 == 0:
                            S.op("act", lambda e, pb=pb, dst=dst, kc=kc: e.activation(
                                out=dst.ap, in_=pb.ap, func=AF.Identity, scale=modv.ap[:, kc:kc + 1], bias=modv.ap[:, 16 + kc:17 + kc]),
                                reads=[pb, modv.all()], writes=[dst])
                        else:
                            S.op("dve", lambda e, pb=pb, dst=dst, kc=kc: e.tensor_scalar(
                                out=dst.ap, in0=pb.ap, scalar1=modv.ap[:, kc:kc + 1], scalar2=modv.ap[:, 16 + kc:17 + kc],
                                op0=ALU.mult, op1=ALU.add), reads=[pb, modv.all()], writes=[dst])
                    if run == 0 and kc % 4 == 3:
                        mod_slot()

            NG = NT // 512
            stage1(0)
            for G in range(NG):
                if G + 1 < NG:
                    stage1(G + 1)
                stage2(G)
                if run == 1 and G == 1:
                    rope_tables()
            if run == 0:
                while mod_steps:
                    mod_slot()
                rope_tables()
                for buf_ in (hT_own, hT_halo):
                    for kc in range(KC):
                        v_ = buf_.v(kc)
                        S.op("act", lambda e, v_=v_, kc=kc: e.activation(
                            out=v_.ap, in_=v_.ap, func=AF.Identity, scale=modv.ap[:, kc:kc + 1], bias=modv.ap[:, 16 + kc:17 + kc]),
                            reads=[v_, modv.all()], writes=[v_])
            rope_tables_act()

            S.op("dve", lambda e: e.tensor_copy(out=hT_h2.ap, in_=hT_halo.ap[:, :, HALO - 2:HALO]),
                 reads=[hT_halo.all()], writes=[hT_h2.all()])

            def rope_evac(pv, dstbuf, d0, n, tabc0):
                dst = dstbuf.v((d0, d0 + n))
                S.op("act", lambda e: e.copy(out=dst.ap, in_=pv.ap), reads=[pv], writes=[dst])
                t1 = rtmp[0]
                t2 = rtmp[1]
                pv32 = View(pv.ap[0:32, :], "ps", pv.lo, pv.hi)
                S.op("dve", lambda e: e.tensor_tensor(out=t1.ap[:, 0:n], in0=pv32.ap, in1=cosT.ap[:, tabc0:tabc0 + n], op=ALU.mult),
                     reads=[pv, cosT.v((tabc0, tabc0 + n))], writes=[t1.v((0, n))])

                def post():
                    bank = S.bank(GEN)
                    pr = S.ps(bank, parts=32, cols=(0, n))
                    d32 = dstbuf.v((d0, d0 + n), p=(0, 32))
                    S.op("pe", lambda e: e.matmul(pr.ap, lhsT=cb.ap[0:32, CB_PM:CB_PM + 32], rhs=d32.ap, start=True, stop=True),
                         reads=[d32, cb.all()], writes=[pr])
                    S.op("dve", lambda e: e.tensor_tensor(out=t2.ap[:, 0:n], in0=pr.ap, in1=sinT.ap[:, tabc0:tabc0 + n], op=ALU.mult),
                         reads=[pr, sinT.v((tabc0, tabc0 + n))], writes=[t2.v((0, n))])
                    S.op("dve", lambda e: e.tensor_tensor(out=d32.ap, in0=t1.ap[:, 0:n], in1=t2.ap[:, 0:n], op=ALU.add),
                         reads=[t1.v((0, n)), t2.v((0, n))], writes=[d32])
                deferred.append(post)

            scale_qk = float(128 ** -0.5)
            for h in range(8):
                for g in range(3):
                    Dg = (1, 4, 16)[g]
                    halo_g = (128, 512, 2048)[g]
                    kc0 = HALO - halo_g
                    wb = w_get(("in", g * 8 + h))
                    for (c0, c1) in tiles(HALO, HALO + CH):
                        pv = proj(wb, c0, c1)
                        rope_evac(pv, qT, c0 - HALO, c1 - c0, c0)
                    wb = w_get(("in", 24 + g * 8 + h))
                    for (c0, c1) in tiles(kc0, HALO + CH):
                        pv = proj(wb, c0, c1)
                        rope_evac(pv, kT, c0, c1 - c0, c0)
                    wb = w_get(("in", 48 + g * 8 + h))
                    for (c0, c1) in tiles(kc0, HALO + CH):
                        pv = proj(wb, c0, c1)
                        dst = vT.v((c0, c1))
                        S.op("act", lambda e, dst=dst, pv=pv: e.copy(out=dst.ap, in_=pv.ap), reads=[pv], writes=[dst])
                    ktiles = []
                    pairs = []
                    if g == 0:
                        for j in range(-1, 8):
                            ktiles.append((HALO + 128 * j, 1, 128))
                        for n_ in range(8):
                            pairs.append((128 * n_, 1, 128, 128 * n_, n_ + 1, n_))
                        masks = [CB_LU4H1 if (run == 0 and b == 0) else CB_LU4 for b in range(4)]
                    elif g == 1:
                        for n_ in range(-1, 2):
                            for r in range(4):
                                ktiles.append((HALO + 512 * n_ + r, 4, 128))
                        for n_ in range(2):
                            for r in range(4):
                                pairs.append((512 * n_ + r, 4, 128, (n_ * 4 + r) * 128, (n_ + 1) * 4 + r, n_ * 4 + r))
                        masks = [CB_LU4H2 if (run == 0 and b < 2) else CB_LU4 for b in range(4)]
                    else:
                        if run == 0:
                            for r in range(16):
                                ktiles.append((r, 16, 128))
                            for r in range(16):
                                ktiles.append((HALO + r, 16, 64))
                            for r in range(16):
                                pairs.append((r, 16, 64, 64 * r, 16 + r, r))
                        else:
                            for r in range(16):
                                ktiles.append((1024 + r, 16, 128))
                            for r in range(16):
                                ktiles.append((r, 16, 64))
                            for r in range(16):
                                pairs.append((r, 16, 64, 64 * r, r, 16 + r))
                        masks = [CB_G2A if run == 0 else CB_G2B] * 4
                    flush_deferred()
                    nkt = len(ktiles)
                    for t0 in range(0, nkt, 8):
                        bank = S.bank(GEN)
                        pb = S.ps(bank, BF16)
                        tl = list(range(t0, min(nkt, t0 + 8)))
                        for t in tl:
                            cs, stp, nk_ = ktiles[t]
                            src = vT.v((cs, cs + (nk_ - 1) * stp + 1, stp))
                            S.op("pe", lambda e, t=t, t0=t0, nk_=nk_, src=src, pb=pb: e.transpose(
                                out=pb.ap[0:nk_, (t - t0) * 128:(t - t0 + 1) * 128], in_=src.ap, identity=identB.ap),
                                reads=[src, identB], writes=[pb], signal=(t == tl[-1]))
                        nkb = ktiles[tl[0]][2]
                        assert all(ktiles[t][2] == nkb for t in tl)
                        dst = Vt.v((t0, t0 + len(tl)), p=(0, nkb))
                        S.op("dve", lambda e, dst=dst, pb=pb, n_=len(tl), nkb=nkb: e.tensor_copy(
                            out=dst.ap, in_=pb.ap[0:nkb, 0:n_ * 128].rearrange("p (a b) -> p a b", a=n_)),
                            reads=[pb], writes=[dst])
                    nq = pairs[0][2]
                    ppb = 512 // (2 * nq)
                    nbk = len(pairs) // ppb
                    assert nbk == 4

                    def score_bank(b):
                        bank = S.bank(GEN)
                        pb = S.ps(bank)
                        for pi in range(ppb):
                            qs, qst, nq_, oc, tc, tp = pairs[b * ppb + pi]
                            qv = qT.v((qs, qs + (nq_ - 1) * qst + 1, qst))
                            for half, t in enumerate((tc, tp)):
                                cs, stp, nk_ = ktiles[t]
                                kv = kT.v((cs, cs + (nk_ - 1) * stp + 1, stp))
                                col = (pi * 2 + half) * nq_ if g < 2 else (pi * 64 if nk_ == 128 else 256 + pi * 64)
                                last = (pi == ppb - 1 and half == 1)
                                S.op("pe", lambda e, kv=kv, qv=qv, col=col, nk_=nk_, nq_=nq_, pb=pb: e.matmul(
                                    pb.ap[0:nk_, col:col + nq_], lhsT=kv.ap, rhs=qv.ap, start=True, stop=True),
                                    reads=[kv, qv], writes=[pb], signal=last)
                        pr = pTb[b % 4]
                        mo = masks[b]
                        segs = [(0, 128, 0, 512)] if g < 2 else [(0, 128, 0, 256), (0, 64, 256, 512)]
                        for (p0, p1, c0_, c1_) in segs:
                            prv = pr.v((c0_, c1_), p=(p0, p1))
                            S.op("act", lambda e, pb=pb, prv=prv, p0=p0, p1=p1, c0_=c0_, c1_=c1_: e.activation(
                                out=prv.ap, in_=pb.ap[p0:p1, c0_:c1_], func=AF.Exp, scale=scale_qk),
                                reads=[pb], writes=[prv])
                            S.op("dve", lambda e, prv=prv, mo=mo, p0=p0, p1=p1, c0_=c0_, c1_=c1_: e.tensor_tensor(
                                out=prv.ap, in0=prv.ap, in1=cb.ap[p0:p1, mo + c0_:mo + c1_], op=ALU.mult),
                                reads=[prv, cb.all()], writes=[prv])
                        return pr

                    def pv_bank(b, pm_):
                        for pi in range(ppb):
                            qs, qst, nq_, oc, tc, tp = pairs[b * ppb + pi]
                            ob = OB[oc // 512]
                            db = DB[oc // 512]
                            oc_ = oc % 512
                            ov = S.ps(ob, cols=(oc_, oc_ + nq_))
                            dv = S.ps(db, cols=(oc_, oc_ + nq_))
                            for half, t in enumerate((tc, tp)):
                                cs, stp, nk_ = ktiles[t]
                                col = (pi * 2 + half) * nq_ if g < 2 else (pi * 64 if nk_ == 128 else 256 + pi * 64)
                                rv = pm_.v((col, col + nq_), p=(0, nk_))
                                vv = Vt.v(t, p=(0, nk_))
                                S.op("pe", lambda e, ov=ov, vv=vv, rv=rv, half=half: e.matmul(
                                    ov.ap, lhsT=vv.ap, rhs=rv.ap, start=(half == 0), stop=(half == 1)),
                                    reads=[vv, rv], writes=[ov], signal=False)
                            for half, t in enumerate((tc, tp)):
                                cs, stp, nk_ = ktiles[t]
                                col = (pi * 2 + half) * nq_ if g < 2 else (pi * 64 if nk_ == 128 else 256 + pi * 64)
                                rv = pm_.v((col, col + nq_), p=(0, nk_))
                                S.op("pe", lambda e, dv=dv, rv=rv, half=half, nk_=nk_: e.matmul(
                                    dv.ap, lhsT=cb.ap[0:nk_, CB_ONES:CB_ONES + 128], rhs=rv.ap, start=(half == 0), stop=(half == 1)),
                                    reads=[cb.all(), rv], writes=[dv], signal=(half == 1))

                    pend = []
                    for b in range(nbk):
                        pend.append((b, score_bank(b)))
                        if len(pend) > 2:
                            pv_bank(*pend.pop(0))
                    while pend:
                        pv_bank(*pend.pop(0))
                    for bi in range(2):
                        ov = S.ps(OB[bi])
                        dv = S.ps(DB[bi])
                        if g == 0:
                            nv = numb.v((512 * bi, 512 * bi + 512))
                            dn = denb.v((512 * bi, 512 * bi + 512))
                            S.op("act", lambda e, nv=nv, ov=ov: e.copy(out=nv.ap, in_=ov.ap), reads=[ov], writes=[nv])
                            S.op("act", lambda e, dn=dn, dv=dv: e.copy(out=dn.ap, in_=dv.ap), reads=[dv], writes=[dn])
                        else:
                            if g == 1:
                                nap = numb.ap[:, 512 * bi:512 * bi + 512].rearrange("p (i r) -> p r i", r=4)
                                dap = denb.ap[:, 512 * bi:512 * bi + 512].rearrange("p (i r) -> p r i", r=4)
                                oap = ov.ap.rearrange("p (r i) -> p r i", r=4)
                                dvp = dv.ap.rearrange("p (r i) -> p r i", r=4)
                                nv = numb.v((512 * bi, 512 * bi + 512))
                                dn = denb.v((512 * bi, 512 * bi + 512))
                            else:
                                nap = numb.ap.rearrange("p (i r) -> p r i", r=16)[:, 8 * bi:8 * bi + 8, :]
                                dap = denb.ap.rearrange("p (i r) -> p r i", r=16)[:, 8 * bi:8 * bi + 8, :]
                                oap = ov.ap.rearrange("p (r i) -> p r i", r=8)
                                dvp = dv.ap.rearrange("p (r i) -> p r i", r=8)
                                nv = numb.all()
                                dn = denb.all()
                            S.op("dve", lambda e, nap=nap, oap=oap: e.tensor_tensor(out=nap, in0=nap, in1=oap, op=ALU.add),
                                 reads=[ov, nv], writes=[nv])
                            S.op("dve", lambda e, dap=dap, dvp=dvp: e.tensor_tensor(out=dap, in0=dap, in1=dvp, op=ALU.add),
                                 reads=[dv, dn], writes=[dn])
                wb = w_get(("in", 72 + h))
                for (c0, c1) in tiles(HALO, HALO + CH):
                    pv = proj(wb, c0, c1)
                    dst = szb.v((c0 - HALO, c1 - HALO))
                    S.op("act", lambda e, dst=dst, pv=pv: e.activation(out=dst.ap, in_=pv.ap, func=AF.Silu), reads=[pv], writes=[dst])
                S.op("dve", lambda e: e.reciprocal(out=denb.ap, in_=denb.ap), reads=[denb.all()], writes=[denb.all()])
                S.op("dve", lambda e: e.tensor_tensor(out=numb.ap, in0=numb.ap, in1=denb.ap, op=ALU.mult),
                     reads=[numb.all(), denb.all()], writes=[numb.all()])
                adst = aT.v(h)
                S.op("dve", lambda e, adst=adst: e.tensor_tensor(out=adst.ap, in0=numb.ap, in1=szb.ap, op=ALU.mult),
                     reads=[numb.all(), szb.all()], writes=[adst])

            for c in range(16):
                if run == 0:
                    gate_slot(5)
                wcc = w_get(("in", 96 + c))
                pvh = proj(wcc, HALO - 2, HALO, src=lambda kc: hT_h2.v(kc))
                S.op("act", lambda e, pvh=pvh: e.copy(out=cch.ap, in_=pvh.ap), reads=[pvh], writes=[cch.all()])
                pcc = []
                for T, (c0, c1) in enumerate(tiles(HALO, HALO + CH)):
                    pv = proj(wcc, c0, c1)
                    S.op("act", lambda e, pv=pv, T=T: e.copy(out=ccs[T].ap, in_=pv.ap), reads=[pv], writes=[ccs[T].all()])
                wcx = w_get(("in", 112 + c))
                pvh = proj(wcx, HALO - 2, HALO, src=lambda kc: hT_h2.v(kc))
                ph = pbuf.v((0, 2))
                S.op("dve", lambda e, pvh=pvh, ph=ph: e.tensor_tensor(out=ph.ap, in0=pvh.ap, in1=cch.ap, op=ALU.mult),
                     reads=[pvh, cch.all()], writes=[ph])
                if run == 0:
                    S.op("dve", lambda e, ph=ph: e.tensor_scalar(out=ph.ap, in0=ph.ap, scalar1=cf.ap[:, CF_HV:CF_HV + 1], scalar2=None,
                                                                 op0=ALU.mult), reads=[ph, cf.all()], writes=[ph])
                for T, (c0, c1) in enumerate(tiles(HALO, HALO + CH)):
                    pv = proj(wcx, c0, c1)
                    pp = pbuf.v((2 + 512 * T, 2 + 512 * T + 512))
                    S.op("dve", lambda e, pv=pv, pp=pp, T=T: e.tensor_tensor(out=pp.ap, in0=pv.ap, in1=ccs[T].ap, op=ALU.mult),
                         reads=[pv, ccs[T].all()], writes=[pp])
                cw0 = CF_CONVW + 3 * c
                S.op("dve", lambda e, cw0=cw0: e.tensor_scalar(out=ubuf.ap, in0=pbuf.ap[:, 2:CH + 2], scalar1=cf.ap[:, cw0 + 2:cw0 + 3],
                                                               scalar2=None, op0=ALU.mult),
                     reads=[pbuf.all(), cf.all()], writes=[ubuf.all()])
                S.op("dve", lambda e, cw0=cw0: e.scalar_tensor_tensor(out=ubuf.ap, in0=pbuf.ap[:, 1:CH + 1], scalar=cf.ap[:, cw0 + 1:cw0 + 2],
                                                                      in1=ubuf.ap, op0=ALU.mult, op1=ALU.add),
                     reads=[pbuf.all(), ubuf.all(), cf.all()], writes=[ubuf.all()])
                S.op("dve", lambda e, cw0=cw0: e.scalar_tensor_tensor(out=ubuf.ap, in0=pbuf.ap[:, 0:CH], scalar=cf.ap[:, cw0:cw0 + 1],
                                                                      in1=ubuf.ap, op0=ALU.mult, op1=ALU.add),
                     reads=[pbuf.all(), ubuf.all(), cf.all()], writes=[ubuf.all()])
                wcb = w_get(("in", 80 + c))
                for T, (c0, c1) in enumerate(tiles(HALO, HALO + CH)):
                    pv = proj(wcb, c0, c1)
                    uu = ubuf.v((512 * T, 512 * T + 512))
                    S.op("dve", lambda e, pv=pv, uu=uu: e.tensor_tensor(out=uu.ap, in0=pv.ap, in1=uu.ap, op=ALU.mult),
                         reads=[pv, uu], writes=[uu])
                wzc = w_get(("in", 128 + c))
                for T, (c0, c1) in enumerate(tiles(HALO, HALO + CH)):
                    pv = proj(wzc, c0, c1)
                    S.op("act", lambda e, pv=pv, T=T: e.activation(out=szc[T].ap, in_=pv.ap, func=AF.Silu),
                         reads=[pv], writes=[szc[T].all()])
                    uu = ubuf.v((512 * T, 512 * T + 512))
                    gd = gcT.v(c, (512 * T, 512 * T + 512))
                    S.op("dve", lambda e, uu=uu, gd=gd, T=T: e.tensor_tensor(out=gd.ap, in0=uu.ap, in1=szc[T].ap, op=ALU.mult),
                         reads=[uu, szc[T].all()], writes=[gd])

            if run == 0:
                gate_slot(1000)

            for f in range(16):
                wa_ = w_get(("ao", f))
                pya = [proj(wa_, HALO + 512 * T, HALO + 512 * T + 512, nk=8,
                            src=(lambda kc, T=T: aT.v(kc, (512 * T, 512 * T + 512))), pool=ALLB) for T in range(2)]
                wg = w_get(("in", 144 + f))
                for T in range(2):
                    pv = proj(wg, HALO + 512 * T, HALO + 512 * T + 512, pool=ALLB)
                    S.op("act", lambda e, pv=pv, T=T: e.activation(out=sg[T].ap, in_=pv.ap, func=AF.Sigmoid),
                         reads=[pv], writes=[sg[T].all()])
                    S.op("dve", lambda e, T=T, py=pya[T]: e.tensor_tensor(out=sg[T].ap, in0=py.ap, in1=sg[T].ap, op=ALU.mult),
                         reads=[pya[T], sg[T].all()], writes=[sg[T].all()])
                wc_ = w_get(("co", f))
                pyc = [proj(wc_, HALO + 512 * T, HALO + 512 * T + 512,
                            src=(lambda kc, T=T: gcT.v(kc, (512 * T, 512 * T + 512))), pool=ALLB) for T in range(2)]
                wg = w_get(("in", 160 + f))
                for T in range(2):
                    pv = proj(wg, HALO + 512 * T, HALO + 512 * T + 512, pool=ALLB)
                    S.op("act", lambda e, pv=pv, T=T: e.activation(out=sg[2 + T].ap, in_=pv.ap, func=AF.Sigmoid),
                         reads=[pv], writes=[sg[2 + T].all()])
                    S.op("dve", lambda e, T=T, py=pyc[T]: e.tensor_tensor(out=sg[2 + T].ap, in0=py.ap, in1=sg[2 + T].ap, op=ALU.mult),
                         reads=[pyc[T], sg[2 + T].all()], writes=[sg[2 + T].all()])
                    md = mT.v(f, (512 * T, 512 * T + 512))
                    S.op("dve", lambda e, T=T, md=md: e.tensor_tensor(out=md.ap, in0=sg[T].ap, in1=sg[2 + T].ap, op=ALU.add),
                         reads=[sg[T].all(), sg[2 + T].all()], writes=[md])

            pss = S.ps(7)

            def first_f(T, f):
                wo_ = w_get(("o", f))
                pv = proj(wo_, HALO + 512 * T, HALO + 512 * T + 512,
                          src=(lambda kc, T=T: mT.v(kc, (512 * T, 512 * T + 512))), pool=ALLB7)
                yd = yTb[T].v(f)
                S.op("act", lambda e: e.copy(out=yd.ap, in_=pv.ap), reads=[pv], writes=[yd])
                sq = sqb[f % 2]
                S.op("act", lambda e: e.activation(out=sq.ap, in_=pv.ap, func=AF.Square), reads=[pv], writes=[sq.all()])

                def ssq_mm():
                    S.op("pe", lambda e: e.matmul(pss.ap, lhsT=onesB.ap, rhs=sq.ap, start=(f == 0), stop=(f == 15)),
                         reads=[onesB, sq.all()], writes=[pss], signal=True)
                deferred.append(ssq_mm)

            def first_end(T):
                flush_deferred()
                rs = rstdb[T]
                S.op("dve", lambda e: e.tensor_scalar(out=rs.ap, in0=pss.ap, scalar1=1.0 / D, scalar2=EPS, op0=ALU.mult, op1=ALU.add),
                     reads=[pss], writes=[rs.all()])
                S.op("act", lambda e: e.activation(out=rs.ap, in_=rs.ap, func=AF.Sqrt), reads=[rs.all()], writes=[rs.all()])
                S.op("dve", lambda e: e.reciprocal(out=rs.ap, in_=rs.ap), reads=[rs.all()], writes=[rs.all()])

            def xload(T):
                row0 = r0 + HALO + 512 * T
                for j in range(4):
                    S.dma("sp", outb.ap[:, j, :], xin[row0 + 128 * j:row0 + 128 * j + 128, :], writes=[outb.v(j)])

            def stt(T, f):
                ot = otmp[f % 2]
                yd = yTb[T].v(f)
                rs = rstdb[T]
                S.op("dve", lambda e: e.scalar_tensor_tensor(out=ot.ap, in0=yd.ap, scalar=modv.ap[:, 32 + f:33 + f],
                                                             in1=rs.ap, op0=ALU.mult, op1=ALU.mult),
                     reads=[yd, rs.all(), modv.all()], writes=[ot.all()])

            def rest(T, f):
                ot = otmp[f % 2]
                bank = S.bank(ALLB7)
                pb = S.ps(bank)
                for j in range(4):
                    S.op("pe", lambda e, j=j: e.transpose(out=pb.ap[:, j * 128:(j + 1) * 128],
                                                          in_=ot.ap[:, j * 128:(j + 1) * 128], identity=identF.ap),
                         reads=[ot.all(), identF], writes=[pb], signal=(j == 3))
                od = outb.v(None, (f * 128, f * 128 + 128))
                S.op("dve", lambda e: e.tensor_tensor(out=od.ap, in0=od.ap,
                                                      in1=pb.ap.rearrange("p (j c) -> p j c", j=4), op=ALU.add),
                     reads=[pb, od], writes=[od])

            def store(T):
                orow = run * CH + 512 * T
                for j in range(4):
                    out_toks.append(S.dma("sp", out_d[orow + 128 * j:orow + 128 * j + 128, :], outb.ap[:, j, :], reads=[outb.v(j)]))

            xload(0)
            for f in range(16):
                first_f(0, f)
            first_end(0)
            stt(0, 0)
            for f in range(16):
                first_f(1, f)
                if f + 1 < 16:
                    stt(0, f + 1)
                rest(0, f)
            first_end(1)
            store(0)
            xload(1)
            stt(1, 0)
            for f in range(16):
                if f + 1 < 16:
                    stt(1, f + 1)
                rest(1, f)
            store(1)

        emit_run(0)
        emit_run(1)
        assert wstate["next"] == len(wseq)
        S.wait_tokens("sp", out_toks)
        with nc.Block() as block:
            S.replay(block)
    return nc


def _blocks(w, kc):
    K, N = w.shape
    nb = N // 128
    a = w.reshape(kc, 128, nb, 128)
    return np.ascontiguousarray(a.transpose(2, 1, 0, 3)).reshape(nb, 128, kc * 128)


def _masks(hv):
    k = np.arange(128)[:, None]
    q = np.arange(128)[None, :]
    LT = (k <= q).astype(np.float32)
    UT = (k >= q).astype(np.float32)
    lu4 = np.concatenate([LT, UT, LT, UT], axis=1)
    lu4h1 = np.concatenate([LT, UT * hv, LT, UT], axis=1)
    lu4h2 = np.concatenate([LT, UT * hv, LT, UT * hv], axis=1)
    g2a = np.concatenate([UT[:, 0:64] * hv] * 4 + [LT[:, 0:64]] * 4, axis=1)
    g2b = np.concatenate([LT[:, 64:128]] * 4 + [UT[:, 0:64] * hv] * 4, axis=1)
    return lu4, lu4h1, lu4h2, g2a, g2b


_PROG = {}


def kernel(x, c, positions, g_pre, w_ada, b_ada, w_in, conv_w, w_attn_o, w_conv_o, w_o, g_post):
    x = np.asarray(x, np.float32)[0]
    pos = np.asarray(positions, np.int32)[0]
    f32 = np.float32
    inv_freq = (500000.0 ** (-(np.arange(0, 32, 2, dtype=np.float64)) / 32.0)).astype(np.float32)
    try:
        import jax.numpy as jnp
        inv_freq = np.asarray(500000.0 ** (-jnp.arange(0, 32, 2, dtype=jnp.float32) / 32), np.float32)
    except Exception:
        pass
    win_r = _blocks(np.asarray(w_in, f32)[0], KC)
    wao_r = _blocks(np.asarray(w_attn_o, f32)[0], 8)
    wco_r = _blocks(np.asarray(w_conv_o, f32)[0], KC)
    wo_r = _blocks(np.asarray(w_o, f32)[0], KC)
    wada_r = np.ascontiguousarray(np.asarray(w_ada, f32)[0].reshape(KC, 128, 3 * D))

    def colT(v, n):
        return np.asarray(v, f32).reshape(n, 128).T

    Pm = np.zeros((128, 32), f32)
    for m in range(32):
        Pm[(m + 16) % 32, m] = 1.0
    in_maps = []
    for core in range(NCORES):
        hv = 0.0 if core == 0 else 1.0
        xin = np.zeros((HALO + TOK, D), f32)
        p32 = np.zeros((HALO + TOK,), np.int32)
        s = core * TOK
        if core > 0:
            xin[:HALO] = x[s - HALO:s]
            p32[:HALO] = pos[s - HALO:s]
        xin[HALO:] = x[s:s + TOK]
        p32[HALO:] = pos[s:s + TOK]
        cstf = np.zeros((128, NCF), f32)
        cstf[:, CF_IDENT:CF_IDENT + 128] = np.eye(128, dtype=f32)
        cstf[:, CF_GPRE:CF_GPRE + 16] = colT(g_pre[0], 16)
        cstf[:, CF_GPOST:CF_GPOST + 16] = colT(g_post[0], 16)
        cstf[:, CF_BADA:CF_BADA + 48] = colT(b_ada[0], 48)
        cstf[:, CF_C:CF_C + 16] = colT(c[0], 16)
        cw = np.asarray(conv_w, f32)[0]
        cstf[:, CF_CONVW:CF_CONVW + 48] = cw.T.reshape(16, 128, 3).transpose(1, 0, 2).reshape(128, 48)
        cstf[0:32, CF_INVF] = np.tile(inv_freq, 2)
        cstf[0:16, CF_SGN] = -1.0
        cstf[16:32, CF_SGN] = 1.0
        cstf[:, CF_ONE] = 1.0
        cstf[:, CF_HV] = hv
        cstf[:, CF_EPS] = EPS
        cstb = np.zeros((128, NCB), f32)
        cstb[:, CB_IDENT:CB_IDENT + 128] = np.eye(128, dtype=f32)
        cstb[:, CB_ONES:CB_ONES + 128] = 1.0
        cstb[:, CB_PM:CB_PM + 32] = Pm
        lu4, lu4h1, lu4h2, g2a, g2b = _masks(hv)
        cstb[:, CB_LU4:CB_LU4 + 512] = lu4
        cstb[:, CB_LU4H1:CB_LU4H1 + 512] = lu4h1
        cstb[:, CB_LU4H2:CB_LU4H2 + 512] = lu4h2
        cstb[:, CB_G2A:CB_G2A + 512] = g2a
        cstb[:, CB_G2B:CB_G2B + 512] = g2b
        in_maps.append({
            "xin": xin, "pos32": np.ascontiguousarray(np.broadcast_to(p32[None, :], (32, HALO + TOK))),
            "cstf": cstf, "cstb": cstb, "wada": wada_r, "win": win_r, "wao": wao_r, "wco": wco_r, "wo": wo_r,
        })
    if "nc" not in _PROG:
        _PROG["nc"] = build_program()
    res = run_bass_kernel_spmd(_PROG["nc"], in_maps, core_ids=list(range(NCORES)))
    out = np.concatenate([np.asarray(r["out"], f32) for r in res.results], axis=0)
    return out.reshape(1, SEQ, D)
```

```python
from contextlib import ExitStack
import numpy as np
import concourse.bass as bass
import concourse.mybir as mybir
from concourse.bass_utils import run_bass_kernel_spmd

F32 = mybir.dt.float32
BF16 = mybir.dt.bfloat16
I32 = mybir.dt.int32
U8 = mybir.dt.uint8
AF = mybir.ActivationFunctionType
ALU = mybir.AluOpType
ESZ = {F32: 4, BF16: 2, I32: 4, U8: 1}

NCORES = 8
D = 2048
SEQ = 16384
TOK = SEQ // NCORES
CH = 1024
HALO = 2048
KC = D // 128
PROJ = 22528
EPS = 1e-6
CELL = 256
ENGS = ("pe", "act", "dve", "pool", "sp")
SBUF_BYTES = 207 * 1024


class View:
    __slots__ = ("ap", "space", "lo", "hi")

    def __init__(self, ap, space, lo, hi):
        self.ap, self.space, self.lo, self.hi = ap, space, lo, hi


class Buf:
    def __init__(self, S, lo, shape, dtype, parts=128):
        self.S, self.lo, self.shape, self.dtype, self.parts = S, lo, list(shape), dtype, parts
        self.esz = ESZ[dtype]
        self.n = int(np.prod(shape))
        ap = S.arena[0:parts, lo:lo + self.n * self.esz]
        if dtype != U8:
            ap = ap.bitcast(dtype)
        if len(shape) == 2:
            ap = ap.rearrange("p (a b) -> p a b", a=shape[0])
        elif len(shape) == 3:
            ap = ap.rearrange("p (a b c) -> p a b c", a=shape[0], b=shape[1])
        self.ap = ap
        self.hi = lo + self.n * self.esz

    def all(self):
        return View(self.ap, "sb", self.lo, self.hi)

    def v(self, *idx, p=None):
        strides = []
        s = 1
        for d in reversed(self.shape):
            strides.insert(0, s)
            s *= d
        key = [slice(None) if p is None else slice(p[0], p[1])]
        lo = 0
        hi = 0
        for i, d in enumerate(self.shape):
            ix = idx[i] if i < len(idx) else None
            if ix is None:
                key.append(slice(None))
                a, b = 0, d
            elif isinstance(ix, int):
                key.append(ix)
                a, b = ix, ix + 1
            else:
                a, b = ix[0], ix[1]
                st = ix[2] if len(ix) > 2 else 1
                key.append(slice(a, b, st) if st != 1 else slice(a, b))
            assert 0 <= a < b <= d, (idx, self.shape)
            lo += a * strides[i]
            hi += (b - 1) * strides[i]
        hi += 1
        return View(self.ap[tuple(key)], "sb", self.lo + lo * self.esz, self.lo + hi * self.esz)


class Sched:
    def __init__(self, nc, stack, sbuf_bytes, n_dma_sems=12):
        self.nc = nc
        self.ops = {e: [] for e in ENGS}
        self.sems = {}
        for e in ENGS:
            self.sems[e] = stack.enter_context(nc.semaphore("s_" + e))
        self.cnt = {e: 0 for e in ENGS}
        self.pending = {e: False for e in ENGS}
        self.known = {e: {} for e in ENGS}
        self.ndma = n_dma_sems
        self.dma_val = {}
        self.dma_rr = {}
        for q in ("sp", "pool", "act"):
            for i in range(n_dma_sems):
                self.sems[("d", q, i)] = stack.enter_context(nc.semaphore("s_dma_%s%d" % (q, i)))
                self.dma_val[(q, i)] = 0
            self.dma_rr[q] = 0
        ncell = (sbuf_bytes + CELL - 1) // CELL
        self.cw = {"sb": [None] * ncell, "ps": [None] * 8}
        self.cr = {"sb": [dict() for _ in range(ncell)], "ps": [dict() for _ in range(8)]}
        self.arena = stack.enter_context(nc.sbuf_tensor("arena", [128, sbuf_bytes], U8))
        self.psum = [stack.enter_context(nc.psum_tensor("pb%d" % i, [128, 512], F32)) for i in range(8)]
        self.sbuf_bytes = sbuf_bytes
        self.rr = {}

    def buf(self, lo, shape, dtype, parts=128):
        b = Buf(self, lo, shape, dtype, parts)
        assert b.hi <= self.sbuf_bytes, "arena overflow"
        return b

    def ps(self, bank, dtype=F32, parts=128, cols=None):
        t = self.psum[bank]
        ap = t[0:parts, :]
        if dtype != F32:
            ap = ap.bitcast(dtype)
        if cols is not None:
            ap = ap[:, cols[0]:cols[1]]
        return View(ap, "ps", bank, bank + 1)

    def bank(self, pool):
        i = self.rr.get(pool, 0)
        self.rr[pool] = (i + 1) % len(pool)
        return pool[i]

    def _cells(self, v):
        if v.space == "ps":
            return range(v.lo, v.hi)
        return range(v.lo // CELL, (v.hi + CELL - 1) // CELL)

    def _deps(self, eng, reads, writes):
        need = {}

        def add(tok):
            if tok is None:
                return
            k, val = tok
            if k == "pe" and eng == "pe":
                return
            if need.get(k, 0) < val:
                need[k] = val

        for v in reads:
            cw, cr = self.cw[v.space], self.cr[v.space]
            for c in self._cells(v):
                add(cw[c])
                if v.space == "ps":
                    for k, val in cr[c].items():
                        if k != eng:
                            add((k, val))
        for v in writes:
            cw, cr = self.cw[v.space], self.cr[v.space]
            for c in self._cells(v):
                add(cw[c])
                for k, val in cr[c].items():
                    add((k, val))
        kn = self.known[eng]
        waits = []
        for k, val in need.items():
            if kn.get(k, 0) >= val:
                continue
            kn[k] = val
            waits.append((k, val))
        return waits

    def _record(self, tok, reads, writes):
        k, val = tok
        for v in reads:
            cr = self.cr[v.space]
            for c in self._cells(v):
                cr[c][k] = val
        for v in writes:
            cw, cr = self.cw[v.space], self.cr[v.space]
            for c in self._cells(v):
                cw[c] = tok
                cr[c] = {}

    def op(self, eng, fn, reads=(), writes=(), signal=True):
        waits = self._deps(eng, reads, writes)
        tok = (eng, self.cnt[eng] + 1)
        if signal:
            self.cnt[eng] += 1
            self.pending[eng] = False
        else:
            self.pending[eng] = True
        self._record(tok, reads, writes)
        self.ops[eng].append((waits, fn, ("inc", eng) if signal else None))
        return tok

    def dma(self, queue, out, in_, reads=(), writes=(), **kw):
        i = self.dma_rr[queue]
        self.dma_rr[queue] = (i + 1) % self.ndma
        waits = self._deps(queue, reads, writes)
        kn = self.known[queue]
        sk = ("d", queue, i)
        if self.dma_val[(queue, i)] > 0 and kn.get(sk, 0) < self.dma_val[(queue, i)]:
            kn[sk] = self.dma_val[(queue, i)]
            waits.append((sk, self.dma_val[(queue, i)]))
        self.dma_val[(queue, i)] += 16
        tok = (sk, self.dma_val[(queue, i)])
        self._record(tok, reads, writes)

        def fn(e, out=out, in_=in_, kw=kw):
            return e.dma_start(out=out, in_=in_, **kw)

        self.ops[queue].append((waits, fn, ("dma", sk)))
        return tok

    def wait_tokens(self, eng, toks):
        kn = self.known[eng]
        waits = []
        for k, val in toks:
            if kn.get(k, 0) < val:
                kn[k] = val
                waits.append((k, val))
        if waits:
            self.ops[eng].append((waits, None, None))

    def replay(self, block):
        sems = self.sems

        def run(e, lst):
            for waits, fn, post in lst:
                for k, val in waits:
                    e.wait_ge(sems[k], val)
                if fn is None:
                    continue
                ins = fn(e)
                if post is None:
                    continue
                if post[0] == "inc":
                    ins.then_inc(sems[post[1]], 1)
                else:
                    ins.then_inc(sems[post[1]], 16)

        for e in ENGS:
            assert not self.pending[e], "engine %s ends with unsignaled instruction" % e

        @block.tensor
        def _(e):
            run(e, self.ops["pe"])

        @block.scalar
        def _(e):
            run(e, self.ops["act"])

        @block.vector
        def _(e):
            run(e, self.ops["dve"])

        @block.gpsimd
        def _(e):
            run(e, self.ops["pool"])

        @block.sync
        def _(e):
            run(e, self.ops["sp"])


CF_IDENT = 0
CF_GPRE = 128
CF_GPOST = 144
CF_BADA = 160
CF_C = 208
CF_CONVW = 224
CF_INVF = 272
CF_SGN = 273
CF_ONE = 274
CF_HV = 275
CF_EPS = 276
NCF = 277
CB_IDENT = 0
CB_ONES = 128
CB_PM = 256
CB_LU4 = 288
CB_LU4H1 = 800
CB_LU4H2 = 1312
CB_G2A = 1824
CB_G2B = 2336
NCB = 2848

GEN = (0, 1, 2, 3)
OB = (4, 5)
DB = (6, 7)
ALLB = (0, 1, 2, 3, 4, 5, 6, 7)
ALLB7 = (0, 1, 2, 3, 4, 5, 6)


def build_program():
    nc = bass.Bass("TRN2", target_bir_lowering=False)
    xin = nc.dram_tensor("xin", [HALO + TOK, D], F32, kind="ExternalInput").ap()
    pos_d = nc.dram_tensor("pos32", [32, HALO + TOK], I32, kind="ExternalInput").ap()
    cf_d = nc.dram_tensor("cstf", [128, NCF], F32, kind="ExternalInput").ap()
    cb_d = nc.dram_tensor("cstb", [128, NCB], F32, kind="ExternalInput").ap()
    wada_d = nc.dram_tensor("wada", [KC, 128, 3 * D], F32, kind="ExternalInput").ap()
    win_d = nc.dram_tensor("win", [PROJ // 128, 128, KC * 128], F32, kind="ExternalInput").ap()
    wao_d = nc.dram_tensor("wao", [16, 128, 8 * 128], F32, kind="ExternalInput").ap()
    wco_d = nc.dram_tensor("wco", [16, 128, KC * 128], F32, kind="ExternalInput").ap()
    wo_d = nc.dram_tensor("wo", [16, 128, KC * 128], F32, kind="ExternalInput").ap()
    out_d = nc.dram_tensor("out", [TOK, D], F32, kind="ExternalOutput").ap()

    with ExitStack() as st:
        S = Sched(nc, st, SBUF_BYTES)
        top = [0]

        def alloc(nbytes):
            lo = (top[0] + CELL - 1) // CELL * CELL
            top[0] = lo + nbytes
            assert top[0] <= SBUF_BYTES, "SBUF overflow %d" % top[0]
            return lo

        def newbuf(shape, dtype, parts=128):
            return S.buf(alloc(int(np.prod(shape)) * ESZ[dtype]), shape, dtype, parts)

        cf = newbuf([NCF], F32)
        cb = newbuf([NCB], BF16)
        modv = newbuf([64], F32)
        modraw = newbuf([64], F32)
        NW = 3
        wbufs = [newbuf([KC, 128], BF16) for _ in range(NW)]
        hT_own = newbuf([KC, CH], BF16)
        aT = newbuf([8, CH], BF16)
        hT_h2 = newbuf([KC, 2], BF16)
        X0 = alloc(0)
        XBYTES = SBUF_BYTES - X0
        assert XBYTES >= 130 * 1024, XBYTES

        def xbuf(off, shape, dtype, parts=128):
            b = S.buf(X0 + off, shape, dtype, parts)
            assert b.hi <= SBUF_BYTES
            return b

        hT_halo = xbuf(0, [KC, HALO], BF16)
        A0 = 64 * 1024
        cosT = xbuf(A0, [CH + HALO], F32, parts=32)
        sinT = xbuf(A0 + 12288, [CH + HALO], F32, parts=32)
        A1 = A0 + 24576
        qT = xbuf(A1, [CH], BF16)
        kT = xbuf(A1 + 2048, [CH + HALO], BF16)
        vT = xbuf(A1 + 8192, [CH + HALO], BF16)
        Vt = xbuf(A1 + 14336, [32, 128], BF16)
        szb = xbuf(A1 + 22528, [CH], F32)
        numb = xbuf(A1 + 26624, [CH], F32)
        denb = xbuf(A1 + 30720, [CH], F32)
        pTb = [xbuf(A1 + 34816 + i * 1024, [512], BF16) for i in range(4)]
        rtmp = [xbuf(A1 + 38912 + i * 2048, [512], F32, parts=32) for i in range(2)]
        A_END = A1 + 43008
        assert A_END <= XBYTES, (A_END, XBYTES)
        xts = [xbuf(A1, [D], F32), S.buf(aT.lo, [D], F32), S.buf(aT.lo + 8192, [D], F32)]
        xh8 = [xbuf(A1 + 8192 + i * 4096, [D], BF16) for i in range(8)]
        stat8 = [xbuf(A1 + 40960 + i * 256, [8], F32) for i in range(8)]
        assert A1 + 40960 + 8 * 256 <= A_END
        rp_k = xbuf(A_END, [1536], F32, parts=32)
        assert A_END + 6144 <= XBYTES
        wa = [xbuf(A0 + i * 2048, [512], F32) for i in range(8)] + [xbuf(A_END + i * 2048, [512], F32) for i in range(4)]
        acc = xbuf(A0 + 16384, [2048], F32)
        assert A_END + 4 * 2048 <= XBYTES
        wa_g = [xbuf(98304 + i * 2048, [512], F32) for i in range(4)]
        acc_g = xbuf(98304 + 8192, [2048], F32)
        gcT = xbuf(0, [KC, CH], BF16)
        mT = xbuf(32768, [KC, CH], BF16)
        yT = xbuf(65536, [KC, 512], F32)
        yTb = [yT, xbuf(0, [KC, 512], F32)]
        outb = xbuf(98304, [4, D], F32)
        ccs = [xbuf(65536 + i * 2048, [512], F32) for i in range(2)]
        pbuf = xbuf(65536 + 4096, [CH + 2], F32)
        ubuf = xbuf(65536 + 4096 + 4352, [CH], F32)
        szc = [xbuf(65536 + 12544 + i * 2048, [512], F32) for i in range(2)]
        cch = xbuf(65536 + 16640, [2], F32)
        sg = [xbuf(65536 + 16896 + i * 2048, [512], F32) for i in range(4)]
        rstd = xbuf(98304 + 32768, [512], F32)
        o0 = 98304 + 32768 + 2048
        sqb = [xbuf(o0 + i * 1024, [512], BF16) for i in range(2)]
        otmp = [xbuf(o0 + 2048 + i * 2048, [512], F32) for i in range(2)]
        rstdb = [rstd, xbuf(o0 + 6144, [512], F32)]
        assert o0 + 8192 <= XBYTES, (o0, XBYTES)
        assert o0 + 2048 + 4096 <= XBYTES, (o0, XBYTES)

        identF = cf.v((CF_IDENT, CF_IDENT + 128))
        identB = cb.v((CB_IDENT, CB_IDENT + 128))
        onesB = cb.v((CB_ONES, CB_ONES + 128))

        def cfcol(c0, n=1, p=None):
            return cf.v((c0, c0 + n), p=p)

        S.dma("sp", cf.ap, cf_d, writes=[cf.all()])
        S.dma("pool", cb.ap[:, 0:1424], cb_d[:, 0:1424], writes=[cb.v((0, 1424))])
        S.dma("pool", cb.ap[:, 1424:NCB], cb_d[:, 1424:NCB], writes=[cb.v((1424, NCB))])

        mod_steps = []
        gate_steps = []

        def mk_third(j, wabufs, accb, queue, steps):
            cnt = [0]
            for kc in range(KC):
                for pc in range(4):
                    def step(kc=kc, pc=pc):
                        w = wabufs[cnt[0] % len(wabufs)]
                        cnt[0] += 1
                        col = j * 2048 + pc * 512
                        S.dma(queue, w.ap, wada_d[kc][:, col:col + 512], writes=[w.all()])
                        av = accb.v((pc * 512, pc * 512 + 512))
                        if kc == 0:
                            S.op("dve", lambda e: e.tensor_scalar(out=av.ap, in0=w.ap, scalar1=cf.ap[:, CF_C:CF_C + 1],
                                                                  scalar2=None, op0=ALU.mult),
                                 reads=[w.all(), cf.all()], writes=[av])
                        else:
                            S.op("dve", lambda e: e.scalar_tensor_tensor(
                                out=av.ap, in0=w.ap, scalar=cf.ap[:, CF_C + kc:CF_C + kc + 1], in1=av.ap,
                                op0=ALU.mult, op1=ALU.add), reads=[w.all(), av, cf.all()], writes=[av])
                    steps.append(step)

            def fin_third():
                pm = S.ps(S.bank(ALLB))
                for i in range(16):
                    S.op("pe", lambda e, i=i: e.matmul(pm.ap[:, i:i + 1], lhsT=accb.ap[:, 128 * i:128 * i + 128],
                                                       rhs=cf.ap[:, CF_ONE:CF_ONE + 1], start=True, stop=True),
                         reads=[accb.all(), cf.all()], writes=[pm], signal=(i == 15))
                b0 = CF_BADA + 16 * j
                mr = modraw.v((16 * j, 16 * j + 16))
                S.op("dve", lambda e: e.tensor_tensor(out=mr.ap, in0=pm.ap[:, 0:16], in1=cf.ap[:, b0:b0 + 16], op=ALU.add),
                     reads=[pm, cf.all()], writes=[modraw.all()])
            steps.append(fin_third)

        mk_third(0, wa, acc, "pool", mod_steps)
        mk_third(1, wa, acc, "pool", mod_steps)

        def fin_gs():
            S.op("dve", lambda e: e.scalar_tensor_tensor(out=modv.ap[:, 0:16], in0=modraw.ap[:, 16:32], scalar=1.0,
                                                         in1=cf.ap[:, CF_GPRE:CF_GPRE + 16], op0=ALU.add, op1=ALU.mult),
                 reads=[modraw.all(), cf.all()], writes=[modv.all()])
            S.op("dve", lambda e: e.tensor_copy(out=modv.ap[:, 16:32], in_=modraw.ap[:, 0:16]),
                 reads=[modraw.all()], writes=[modv.all()])
        mod_steps.append(fin_gs)

        mk_third(2, wa_g, acc_g, "sp", gate_steps)

        def fin_gg():
            S.op("dve", lambda e: e.tensor_tensor(out=modv.ap[:, 32:48], in0=modraw.ap[:, 32:48],
                                                  in1=cf.ap[:, CF_GPOST:CF_GPOST + 16], op=ALU.mult),
                 reads=[modraw.all(), cf.all()], writes=[modv.all()])
        gate_steps.append(fin_gg)

        def gate_slot(n):
            for _ in range(n):
                if gate_steps:
                    gate_steps.pop(0)()

        def mod_slot(n=4):
            for _ in range(n):
                if mod_steps:
                    mod_steps.pop(0)()

        wseq = []
        for run in range(2):
            for h in range(8):
                for g in range(3):
                    wseq += [("in", g * 8 + h), ("in", 24 + g * 8 + h), ("in", 48 + g * 8 + h)]
                wseq.append(("in", 72 + h))
            for c in range(16):
                wseq += [("in", 96 + c), ("in", 112 + c), ("in", 80 + c), ("in", 128 + c)]
            for f in range(16):
                wseq += [("ao", f), ("in", 144 + f), ("co", f), ("in", 160 + f)]
            for T in range(2):
                for f in range(16):
                    wseq.append(("o", f))
        wstate = {"issued": 0, "next": 0}

        def w_issue(j):
            kind, idx = wseq[j]
            b = wbufs[j % NW]
            if kind == "ao":
                S.dma("pool", b.ap[:, 0:8, :], wao_d[idx].rearrange("p (a b) -> p a b", a=8), writes=[b.all()])
            else:
                src = {"in": win_d, "co": wco_d, "o": wo_d}[kind][idx]
                S.dma("pool", b.ap, src.rearrange("p (a b) -> p a b", a=KC), writes=[b.all()])

        def w_get(expect):
            i = wstate["next"]
            assert wseq[i] == expect, (i, wseq[i], expect)
            wstate["next"] = i + 1
            while wstate["issued"] < min(len(wseq), i + NW):
                w_issue(wstate["issued"])
                wstate["issued"] += 1
            return wbufs[i % NW]

        deferred = []

        def flush_deferred():
            lst = deferred[:]
            del deferred[:]
            for fn in lst:
                fn()

        def hT_src(c0, c1):
            if c1 <= HALO:
                return lambda kc: hT_halo.v(kc, (c0, c1))
            assert c0 >= HALO
            return lambda kc: hT_own.v(kc, (c0 - HALO, c1 - HALO))

        def proj(wb, c0, c1, nk=KC, src=None, pool=GEN):
            n = c1 - c0
            bank = S.bank(pool)
            pv = S.ps(bank, cols=(0, n))
            srcf = src if src is not None else hT_src(c0, c1)
            for kc in range(nk):
                r = srcf(kc)
                S.op("pe", lambda e, kc=kc, r=r: e.matmul(pv.ap, lhsT=wb.ap[:, kc, :], rhs=r.ap,
                                                          start=(kc == 0), stop=(kc == nk - 1)),
                     reads=[wb.v(kc), r], writes=[pv], signal=(kc == nk - 1))
            flush_deferred()
            return pv

        def tiles(c0, c1):
            res = []
            c = c0
            while c < c1:
                lim = HALO if c < HALO else c1
                e_ = min(c + 512, lim, c1)
                res.append((c, e_))
                c = e_
            return res

        out_toks = []

        def emit_run(run):
            r0 = run * CH
            NT = CH + HALO

            def rope_tables():
                inv2pi = float(np.float32(1.0 / (2.0 * np.pi)))
                MAGIC = 12582912.0
                C1 = 6.28125
                C2 = float(np.float32(2.0 * np.pi - 6.28125))
                PI_LO = 3.1415925
                def half(hf):
                    a, b = hf * 1536, hf * 1536 + 1536
                    sv = sinT.v((a, b))
                    cv = cosT.v((a, b))
                    kv_ = rp_k.all()
                    S.dma("pool", sv.ap, pos_d[:, r0 + a:r0 + b], writes=[sv])
                    S.op("dve", lambda e: e.tensor_scalar(out=sv.ap, in0=sv.ap, scalar1=cf.ap[0:32, CF_INVF:CF_INVF + 1],
                                                           scalar2=None, op0=ALU.mult), reads=[sv, cf.all()], writes=[sv])
                    S.op("dve", lambda e: e.tensor_scalar(out=kv_.ap, in0=sv.ap, scalar1=inv2pi, scalar2=MAGIC,
                                                           op0=ALU.mult, op1=ALU.add), reads=[sv], writes=[kv_])
                    S.op("dve", lambda e: e.tensor_scalar(out=kv_.ap, in0=kv_.ap, scalar1=MAGIC, scalar2=None,
                                                           op0=ALU.subtract), reads=[kv_], writes=[kv_])
                    for Cc in (C1, C2):
                        S.op("dve", lambda e, Cc=Cc: e.tensor_scalar(out=cv.ap, in0=kv_.ap, scalar1=-Cc, scalar2=None, op0=ALU.mult),
                             reads=[kv_], writes=[cv])
                        S.op("dve", lambda e: e.tensor_tensor(out=sv.ap, in0=sv.ap, in1=cv.ap, op=ALU.add),
                             reads=[sv, cv], writes=[sv])
                    S.op("dve", lambda e: e.tensor_scalar(out=sv.ap, in0=sv.ap, scalar1=PI_LO, scalar2=-PI_LO,
                                                          op0=ALU.min, op1=ALU.max), reads=[sv], writes=[sv])
                half(0)
                half(1)

            def rope_tables_act():
                def half(hf):
                    a, b = hf * 1536, hf * 1536 + 1536
                    sv = sinT.v((a, b))
                    cv = cosT.v((a, b))
                    S.op("act", lambda e: e.activation(out=cv.ap, in_=sv.ap, func=AF.Sin, scale=0.5), reads=[sv], writes=[cv])
                    S.op("act", lambda e: e.activation(out=sv.ap, in_=sv.ap, func=AF.Sin, scale=cf.ap[0:32, CF_SGN:CF_SGN + 1]),
                         reads=[sv, cf.all()], writes=[sv])
                    S.op("dve", lambda e: e.tensor_tensor(out=cv.ap, in0=cv.ap, in1=cv.ap, op=ALU.mult), reads=[cv], writes=[cv])
                    S.op("dve", lambda e: e.tensor_scalar(out=cv.ap, in0=cv.ap, scalar1=-2.0, scalar2=1.0,
                                                           op0=ALU.mult, op1=ALU.add), reads=[cv], writes=[cv])
                half(0)
                half(1)


            tile_ctr = [0]

            def stage1(G):
                def fin(i, x_, st):
                    S.op("act", lambda e: e.activation(out=st.ap[:, 2:3], in_=st.ap[:, 0:1], func=AF.Sqrt, scale=1.0 / D, bias=cf.ap[:, CF_EPS:CF_EPS + 1]),
                         reads=[st.all(), cf.all()], writes=[st.all()])
                    S.op("dve", lambda e: e.reciprocal(out=st.ap[:, 3:4], in_=st.ap[:, 2:3]),
                         reads=[st.all()], writes=[st.all()])
                    xh_ = xh8[(G % 2) * 4 + i]
                    S.op("dve", lambda e: e.tensor_scalar(out=xh_.ap, in0=x_.ap, scalar1=st.ap[:, 3:4],
                                                          scalar2=None, op0=ALU.mult),
                         reads=[x_.all(), st.all()], writes=[xh_.all()])
                prevt = None
                for i in range(4):
                    row = r0 + G * 512 + i * 128
                    x_ = xts[tile_ctr[0] % 3]
                    tile_ctr[0] += 1
                    st = stat8[(G % 2) * 4 + i]
                    xh_ = xh8[(G % 2) * 4 + i]
                    S.dma("sp", x_.ap, xin[row:row + 128, :], writes=[x_.all()])
                    S.op("act", lambda e, x_=x_, st=st, xh_=xh_: e.activation(out=xh_.ap, in_=x_.ap, func=AF.Square,
                                                                           accum_out=st.ap[:, 0:1]),
                         reads=[x_.all()], writes=[xh_.all(), st.all()])
                    if prevt is not None:
                        fin(*prevt)
                    prevt = (i, x_, st)
                    if run == 0:
                        mod_slot()
                fin(*prevt)

            def stage2(G):
                for kc in range(KC):
                    bank = S.bank(ALLB)
                    pb = S.ps(bank, BF16, cols=(0, 512))
                    for i in range(4):
                        xh_ = xh8[(G % 2) * 4 + i]
                        S.op("pe", lambda e, i=i, kc=kc, pb=pb, xh_=xh_: e.transpose(out=pb.ap[:, i * 128:(i + 1) * 128],
                                                                                   in_=xh_.ap[:, kc * 128:(kc + 1) * 128],
                                                                                   identity=identB.ap),
                             reads=[xh_.v((kc * 128, kc * 128 + 128)), identB], writes=[pb], signal=(i == 3))
                    c0 = G * 512
                    dst = hT_halo.v(kc, (c0, c0 + 512)) if c0 < HALO else hT_own.v(kc, (c0 - HALO, c0 - HALO + 512))
                    if run == 0:
                        if kc % 2 == 0:
                            S.op("act", lambda e, pb=pb, dst=dst: e.copy(out=dst.ap, in_=pb.ap), reads=[pb], writes=[dst])
                        else:
                            S.op("dve", lambda e, pb=pb, dst=dst: e.tensor_copy(out=dst.ap, in_=pb.ap), reads=[pb], writes=[dst])
                    else:
                        if kc % 2 == 0:
                            S.op("act", lambda e, pb=pb, dst=dst, kc=kc: e.activation(
                                out=dst.ap, in_=pb.ap, func=AF.Identity, scale=modv.ap[:, kc:kc + 1], bias=modv.ap[:, 16 + kc:17 + kc]),
                                reads=[pb, modv.all()], writes=[dst])
                        else:
                            S.op("dve", lambda e, pb=pb, dst=dst, kc=kc: e.tensor_scalar(
                                out=dst.ap, in0=pb.ap, scalar1=modv.ap[:, kc:kc + 1], scalar2=modv.ap[:, 16 + kc:17 + kc],
                                op0=ALU.mult, op1=ALU.add), reads=[pb, modv.all()], writes=[dst])
                    if run == 0 and kc % 4 == 3:
                        mod_slot()

            NG = NT // 512
            if run == 0:
                glist = list(range(NG))
            else:
                glist = [0, 1, 4, 5]
                for kc in range(KC):
                    src_ = hT_own.v(kc)
                    dst_ = hT_halo.v(kc, (1024, 2048))
                    if kc % 2 == 0:
                        S.op("act", lambda e, src_=src_, dst_=dst_: e.copy(out=dst_.ap, in_=src_.ap), reads=[src_], writes=[dst_])
                    else:
                        S.op("dve", lambda e, src_=src_, dst_=dst_: e.tensor_copy(out=dst_.ap, in_=src_.ap), reads=[src_], writes=[dst_])
            stage1(glist[0])
            for gi, G in enumerate(glist):
                if gi + 1 < len(glist):
                    stage1(glist[gi + 1])
                stage2(G)
                if run == 1 and gi == 1:
                    rope_tables()
            if run == 0:
                while mod_steps:
                    mod_slot()
                rope_tables()
                for buf_ in (hT_own, hT_halo):
                    for kc in range(KC):
                        v_ = buf_.v(kc)
                        S.op("act", lambda e, v_=v_, kc=kc: e.activation(
                            out=v_.ap, in_=v_.ap, func=AF.Identity, scale=modv.ap[:, kc:kc + 1], bias=modv.ap[:, 16 + kc:17 + kc]),
                            reads=[v_, modv.all()], writes=[v_])
            rope_tables_act()

            S.op("dve", lambda e: e.tensor_copy(out=hT_h2.ap, in_=hT_halo.ap[:, :, HALO - 2:HALO]),
                 reads=[hT_halo.all()], writes=[hT_h2.all()])

            def rope_evac(pv, dstbuf, d0, n, tabc0):
                dst = dstbuf.v((d0, d0 + n))
                S.op("act", lambda e: e.copy(out=dst.ap, in_=pv.ap), reads=[pv], writes=[dst])
                t1 = rtmp[0]
                t2 = rtmp[1]
                pv32 = View(pv.ap[0:32, :], "ps", pv.lo, pv.hi)
                S.op("dve", lambda e: e.tensor_tensor(out=t1.ap[:, 0:n], in0=pv32.ap, in1=cosT.ap[:, tabc0:tabc0 + n], op=ALU.mult),
                     reads=[pv, cosT.v((tabc0, tabc0 + n))], writes=[t1.v((0, n))])

                def post():
                    bank = S.bank(GEN)
                    pr = S.ps(bank, parts=32, cols=(0, n))
                    d32 = dstbuf.v((d0, d0 + n), p=(0, 32))
                    S.op("pe", lambda e: e.matmul(pr.ap, lhsT=cb.ap[0:32, CB_PM:CB_PM + 32], rhs=d32.ap, start=True, stop=True),
                         reads=[d32, cb.all()], writes=[pr])
                    S.op("dve", lambda e: e.tensor_tensor(out=t2.ap[:, 0:n], in0=pr.ap, in1=sinT.ap[:, tabc0:tabc0 + n], op=ALU.mult),
                         reads=[pr, sinT.v((tabc0, tabc0 + n))], writes=[t2.v((0, n))])
                    S.op("dve", lambda e: e.tensor_tensor(out=d32.ap, in0=t1.ap[:, 0:n], in1=t2.ap[:, 0:n], op=ALU.add),
                         reads=[t1.v((0, n)), t2.v((0, n))], writes=[d32])
                deferred.append(post)

            scale_qk = float(128 ** -0.5)
            for h in range(8):
                for g in range(3):
                    Dg = (1, 4, 16)[g]
                    halo_g = (128, 512, 2048)[g]
                    kc0 = HALO - halo_g
                    wb = w_get(("in", g * 8 + h))
                    for (c0, c1) in tiles(HALO, HALO + CH):
                        pv = proj(wb, c0, c1)
                        rope_evac(pv, qT, c0 - HALO, c1 - c0, c0)
                    wb = w_get(("in", 24 + g * 8 + h))
                    for (c0, c1) in tiles(kc0, HALO + CH):
                        pv = proj(wb, c0, c1)
                        rope_evac(pv, kT, c0, c1 - c0, c0)
                    wb = w_get(("in", 48 + g * 8 + h))
                    for (c0, c1) in tiles(kc0, HALO + CH):
                        pv = proj(wb, c0, c1)
                        dst = vT.v((c0, c1))
                        S.op("act", lambda e, dst=dst, pv=pv: e.copy(out=dst.ap, in_=pv.ap), reads=[pv], writes=[dst])
                    ktiles = []
                    pairs = []
                    if g == 0:
                        for j in range(-1, 8):
                            ktiles.append((HALO + 128 * j, 1, 128))
                        for n_ in range(8):
                            pairs.append((128 * n_, 1, 128, 128 * n_, n_ + 1, n_))
                        masks = [CB_LU4H1 if (run == 0 and b == 0) else CB_LU4 for b in range(4)]
                    elif g == 1:
                        for n_ in range(-1, 2):
                            for r in range(4):
                                ktiles.append((HALO + 512 * n_ + r, 4, 128))
                        for n_ in range(2):
                            for r in range(4):
                                pairs.append((512 * n_ + r, 4, 128, (n_ * 4 + r) * 128, (n_ + 1) * 4 + r, n_ * 4 + r))
                        masks = [CB_LU4H2 if (run == 0 and b < 2) else CB_LU4 for b in range(4)]
                    else:
                        if run == 0:
                            for r in range(16):
                                ktiles.append((r, 16, 128))
                            for r in range(16):
                                ktiles.append((HALO + r, 16, 64))
                            for r in range(16):
                                pairs.append((r, 16, 64, 64 * r, 16 + r, r))
                        else:
                            for r in range(16):
                                ktiles.append((1024 + r, 16, 128))
                            for r in range(16):
                                ktiles.append((r, 16, 64))
                            for r in range(16):
                                pairs.append((r, 16, 64, 64 * r, r, 16 + r))
                        masks = [CB_G2A if run == 0 else CB_G2B] * 4
                    flush_deferred()
                    nkt = len(ktiles)
                    for t0 in range(0, nkt, 8):
                        bank = S.bank(GEN)
                        pb = S.ps(bank, BF16)
                        tl = list(range(t0, min(nkt, t0 + 8)))
                        for t in tl:
                            cs, stp, nk_ = ktiles[t]
                            src = vT.v((cs, cs + (nk_ - 1) * stp + 1, stp))
                            S.op("pe", lambda e, t=t, t0=t0, nk_=nk_, src=src, pb=pb: e.transpose(
                                out=pb.ap[0:nk_, (t - t0) * 128:(t - t0 + 1) * 128], in_=src.ap, identity=identB.ap),
                                reads=[src, identB], writes=[pb], signal=(t == tl[-1]))
                        nkb = ktiles[tl[0]][2]
                        assert all(ktiles[t][2] == nkb for t in tl)
                        dst = Vt.v((t0, t0 + len(tl)), p=(0, nkb))
                        S.op("dve", lambda e, dst=dst, pb=pb, n_=len(tl), nkb=nkb: e.tensor_copy(
                            out=dst.ap, in_=pb.ap[0:nkb, 0:n_ * 128].rearrange("p (a b) -> p a b", a=n_)),
                            reads=[pb], writes=[dst])
                    nq = pairs[0][2]
                    ppb = 512 // (2 * nq)
                    nbk = len(pairs) // ppb
                    assert nbk == 4

                    def score_bank(b):
                        bank = S.bank(GEN)
                        pb = S.ps(bank)
                        for pi in range(ppb):
                            qs, qst, nq_, oc, tc, tp = pairs[b * ppb + pi]
                            qv = qT.v((qs, qs + (nq_ - 1) * qst + 1, qst))
                            for half, t in enumerate((tc, tp)):
                                cs, stp, nk_ = ktiles[t]
                                kv = kT.v((cs, cs + (nk_ - 1) * stp + 1, stp))
                                col = (pi * 2 + half) * nq_ if g < 2 else (pi * 64 if nk_ == 128 else 256 + pi * 64)
                                last = (pi == ppb - 1 and half == 1)
                                S.op("pe", lambda e, kv=kv, qv=qv, col=col, nk_=nk_, nq_=nq_, pb=pb: e.matmul(
                                    pb.ap[0:nk_, col:col + nq_], lhsT=kv.ap, rhs=qv.ap, start=True, stop=True),
                                    reads=[kv, qv], writes=[pb], signal=last)
                        pr = pTb[b % 4]
                        mo = masks[b]
                        segs = [(0, 128, 0, 512)] if g < 2 else [(0, 128, 0, 256), (0, 64, 256, 512)]
                        for (p0, p1, c0_, c1_) in segs:
                            prv = pr.v((c0_, c1_), p=(p0, p1))
                            S.op("act", lambda e, pb=pb, prv=prv, p0=p0, p1=p1, c0_=c0_, c1_=c1_: e.activation(
                                out=prv.ap, in_=pb.ap[p0:p1, c0_:c1_], func=AF.Exp, scale=scale_qk),
                                reads=[pb], writes=[prv])
                            S.op("dve", lambda e, prv=prv, mo=mo, p0=p0, p1=p1, c0_=c0_, c1_=c1_: e.tensor_tensor(
                                out=prv.ap, in0=prv.ap, in1=cb.ap[p0:p1, mo + c0_:mo + c1_], op=ALU.mult),
                                reads=[prv, cb.all()], writes=[prv])
                        return pr

                    def pv_bank(b, pm_):
                        for pi in range(ppb):
                            qs, qst, nq_, oc, tc, tp = pairs[b * ppb + pi]
                            ob = OB[oc // 512]
                            db = DB[oc // 512]
                            oc_ = oc % 512
                            ov = S.ps(ob, cols=(oc_, oc_ + nq_))
                            dv = S.ps(db, cols=(oc_, oc_ + nq_))
                            for half, t in enumerate((tc, tp)):
                                cs, stp, nk_ = ktiles[t]
                                col = (pi * 2 + half) * nq_ if g < 2 else (pi * 64 if nk_ == 128 else 256 + pi * 64)
                                rv = pm_.v((col, col + nq_), p=(0, nk_))
                                vv = Vt.v(t, p=(0, nk_))
                                S.op("pe", lambda e, ov=ov, vv=vv, rv=rv, half=half: e.matmul(
                                    ov.ap, lhsT=vv.ap, rhs=rv.ap, start=(half == 0), stop=(half == 1)),
                                    reads=[vv, rv], writes=[ov], signal=False)
                            for half, t in enumerate((tc, tp)):
                                cs, stp, nk_ = ktiles[t]
                                col = (pi * 2 + half) * nq_ if g < 2 else (pi * 64 if nk_ == 128 else 256 + pi * 64)
                                rv = pm_.v((col, col + nq_), p=(0, nk_))
                                S.op("pe", lambda e, dv=dv, rv=rv, half=half, nk_=nk_: e.matmul(
                                    dv.ap, lhsT=cb.ap[0:nk_, CB_ONES:CB_ONES + 128], rhs=rv.ap, start=(half == 0), stop=(half == 1)),
                                    reads=[cb.all(), rv], writes=[dv], signal=(half == 1))

                    pend = []
                    for b in range(nbk):
                        pend.append((b, score_bank(b)))
                        if len(pend) > 2:
                            pv_bank(*pend.pop(0))
                    while pend:
                        pv_bank(*pend.pop(0))
                    for bi in range(2):
                        ov = S.ps(OB[bi])
                        dv = S.ps(DB[bi])
                        if g == 0:
                            nv = numb.v((512 * bi, 512 * bi + 512))
                            dn = denb.v((512 * bi, 512 * bi + 512))
                            S.op("act", lambda e, nv=nv, ov=ov: e.copy(out=nv.ap, in_=ov.ap), reads=[ov], writes=[nv])
                            S.op("act", lambda e, dn=dn, dv=dv: e.copy(out=dn.ap, in_=dv.ap), reads=[dv], writes=[dn])
                        else:
                            if g == 1:
                                nap = numb.ap[:, 512 * bi:512 * bi + 512].rearrange("p (i r) -> p r i", r=4)
                                dap = denb.ap[:, 512 * bi:512 * bi + 512].rearrange("p (i r) -> p r i", r=4)
                                oap = ov.ap.rearrange("p (r i) -> p r i", r=4)
                                dvp = dv.ap.rearrange("p (r i) -> p r i", r=4)
                                nv = numb.v((512 * bi, 512 * bi + 512))
                                dn = denb.v((512 * bi, 512 * bi + 512))
                            else:
                                nap = numb.ap.rearrange("p (i r) -> p r i", r=16)[:, 8 * bi:8 * bi + 8, :]
                                dap = denb.ap.rearrange("p (i r) -> p r i", r=16)[:, 8 * bi:8 * bi + 8, :]
                                oap = ov.ap.rearrange("p (r i) -> p r i", r=8)
                                dvp = dv.ap.rearrange("p (r i) -> p r i", r=8)
                                nv = numb.all()
                                dn = denb.all()
                            S.op("dve", lambda e, nap=nap, oap=oap: e.tensor_tensor(out=nap, in0=nap, in1=oap, op=ALU.add),
                                 reads=[ov, nv], writes=[nv])
                            S.op("dve", lambda e, dap=dap, dvp=dvp: e.tensor_tensor(out=dap, in0=dap, in1=dvp, op=ALU.add),
                                 reads=[dv, dn], writes=[dn])
                wb = w_get(("in", 72 + h))
                for (c0, c1) in tiles(HALO, HALO + CH):
                    pv = proj(wb, c0, c1)
                    dst = szb.v((c0 - HALO, c1 - HALO))
                    S.op("act", lambda e, dst=dst, pv=pv: e.activation(out=dst.ap, in_=pv.ap, func=AF.Silu), reads=[pv], writes=[dst])
                S.op("dve", lambda e: e.reciprocal(out=denb.ap, in_=denb.ap), reads=[denb.all()], writes=[denb.all()])
                S.op("dve", lambda e: e.tensor_tensor(out=numb.ap, in0=numb.ap, in1=denb.ap, op=ALU.mult),
                     reads=[numb.all(), denb.all()], writes=[numb.all()])
                adst = aT.v(h)
                S.op("dve", lambda e, adst=adst: e.tensor_tensor(out=adst.ap, in0=numb.ap, in1=szb.ap, op=ALU.mult),
                     reads=[numb.all(), szb.all()], writes=[adst])

            for c in range(16):
                if run == 0:
                    gate_slot(5)
                wcc = w_get(("in", 96 + c))
                pvh = proj(wcc, HALO - 2, HALO, src=lambda kc: hT_h2.v(kc))
                S.op("act", lambda e, pvh=pvh: e.copy(out=cch.ap, in_=pvh.ap), reads=[pvh], writes=[cch.all()])
                pcc = []
                for T, (c0, c1) in enumerate(tiles(HALO, HALO + CH)):
                    pv = proj(wcc, c0, c1)
                    S.op("act", lambda e, pv=pv, T=T: e.copy(out=ccs[T].ap, in_=pv.ap), reads=[pv], writes=[ccs[T].all()])
                wcx = w_get(("in", 112 + c))
                pvh = proj(wcx, HALO - 2, HALO, src=lambda kc: hT_h2.v(kc))
                ph = pbuf.v((0, 2))
                S.op("dve", lambda e, pvh=pvh, ph=ph: e.tensor_tensor(out=ph.ap, in0=pvh.ap, in1=cch.ap, op=ALU.mult),
                     reads=[pvh, cch.all()], writes=[ph])
                if run == 0:
                    S.op("dve", lambda e, ph=ph: e.tensor_scalar(out=ph.ap, in0=ph.ap, scalar1=cf.ap[:, CF_HV:CF_HV + 1], scalar2=None,
                                                                 op0=ALU.mult), reads=[ph, cf.all()], writes=[ph])
                for T, (c0, c1) in enumerate(tiles(HALO, HALO + CH)):
                    pv = proj(wcx, c0, c1)
                    pp = pbuf.v((2 + 512 * T, 2 + 512 * T + 512))
                    S.op("dve", lambda e, pv=pv, pp=pp, T=T: e.tensor_tensor(out=pp.ap, in0=pv.ap, in1=ccs[T].ap, op=ALU.mult),
                         reads=[pv, ccs[T].all()], writes=[pp])
                cw0 = CF_CONVW + 3 * c
                S.op("dve", lambda e, cw0=cw0: e.tensor_scalar(out=ubuf.ap, in0=pbuf.ap[:, 2:CH + 2], scalar1=cf.ap[:, cw0 + 2:cw0 + 3],
                                                               scalar2=None, op0=ALU.mult),
                     reads=[pbuf.all(), cf.all()], writes=[ubuf.all()])
                S.op("dve", lambda e, cw0=cw0: e.scalar_tensor_tensor(out=ubuf.ap, in0=pbuf.ap[:, 1:CH + 1], scalar=cf.ap[:, cw0 + 1:cw0 + 2],
                                                                      in1=ubuf.ap, op0=ALU.mult, op1=ALU.add),
                     reads=[pbuf.all(), ubuf.all(), cf.all()], writes=[ubuf.all()])
                S.op("dve", lambda e, cw0=cw0: e.scalar_tensor_tensor(out=ubuf.ap, in0=pbuf.ap[:, 0:CH], scalar=cf.ap[:, cw0:cw0 + 1],
                                                                      in1=ubuf.ap, op0=ALU.mult, op1=ALU.add),
                     reads=[pbuf.all(), ubuf.all(), cf.all()], writes=[ubuf.all()])
                wcb = w_get(("in", 80 + c))
                for T, (c0, c1) in enumerate(tiles(HALO, HALO + CH)):
                    pv = proj(wcb, c0, c1)
                    uu = ubuf.v((512 * T, 512 * T + 512))
                    S.op("dve", lambda e, pv=pv, uu=uu: e.tensor_tensor(out=uu.ap, in0=pv.ap, in1=uu.ap, op=ALU.mult),
                         reads=[pv, uu], writes=[uu])
                wzc = w_get(("in", 128 + c))
                for T, (c0, c1) in enumerate(tiles(HALO, HALO + CH)):
                    pv = proj(wzc, c0, c1)
                    S.op("act", lambda e, pv=pv, T=T: e.activation(out=szc[T].ap, in_=pv.ap, func=AF.Silu),
                         reads=[pv], writes=[szc[T].all()])
                    uu = ubuf.v((512 * T, 512 * T + 512))
                    gd = gcT.v(c, (512 * T, 512 * T + 512))
                    S.op("dve", lambda e, uu=uu, gd=gd, T=T: e.tensor_tensor(out=gd.ap, in0=uu.ap, in1=szc[T].ap, op=ALU.mult),
                         reads=[uu, szc[T].all()], writes=[gd])

            if run == 0:
                gate_slot(1000)

            for f in range(16):
                wa_ = w_get(("ao", f))
                pya = [proj(wa_, HALO + 512 * T, HALO + 512 * T + 512, nk=8,
                            src=(lambda kc, T=T: aT.v(kc, (512 * T, 512 * T + 512))), pool=ALLB) for T in range(2)]
                wg = w_get(("in", 144 + f))
                for T in range(2):
                    pv = proj(wg, HALO + 512 * T, HALO + 512 * T + 512, pool=ALLB)
                    S.op("act", lambda e, pv=pv, T=T: e.activation(out=sg[T].ap, in_=pv.ap, func=AF.Sigmoid),
                         reads=[pv], writes=[sg[T].all()])
                    S.op("dve", lambda e, T=T, py=pya[T]: e.tensor_tensor(out=sg[T].ap, in0=py.ap, in1=sg[T].ap, op=ALU.mult),
                         reads=[pya[T], sg[T].all()], writes=[sg[T].all()])
                wc_ = w_get(("co", f))
                pyc = [proj(wc_, HALO + 512 * T, HALO + 512 * T + 512,
                            src=(lambda kc, T=T: gcT.v(kc, (512 * T, 512 * T + 512))), pool=ALLB) for T in range(2)]
                wg = w_get(("in", 160 + f))
                for T in range(2):
                    pv = proj(wg, HALO + 512 * T, HALO + 512 * T + 512, pool=ALLB)
                    S.op("act", lambda e, pv=pv, T=T: e.activation(out=sg[2 + T].ap, in_=pv.ap, func=AF.Sigmoid),
                         reads=[pv], writes=[sg[2 + T].all()])
                    S.op("dve", lambda e, T=T, py=pyc[T]: e.tensor_tensor(out=sg[2 + T].ap, in0=py.ap, in1=sg[2 + T].ap, op=ALU.mult),
                         reads=[pyc[T], sg[2 + T].all()], writes=[sg[2 + T].all()])
                    md = mT.v(f, (512 * T, 512 * T + 512))
                    S.op("dve", lambda e, T=T, md=md: e.tensor_tensor(out=md.ap, in0=sg[T].ap, in1=sg[2 + T].ap, op=ALU.add),
                         reads=[sg[T].all(), sg[2 + T].all()], writes=[md])

            pss = S.ps(7)

            def first_f(T, f):
                wo_ = w_get(("o", f))
                pv = proj(wo_, HALO + 512 * T, HALO + 512 * T + 512,
                          src=(lambda kc, T=T: mT.v(kc, (512 * T, 512 * T + 512))), pool=ALLB7)
                yd = yTb[T].v(f)
                S.op("act", lambda e: e.copy(out=yd.ap, in_=pv.ap), reads=[pv], writes=[yd])
                sq = sqb[f % 2]
                S.op("act", lambda e: e.activation(out=sq.ap, in_=pv.ap, func=AF.Square), reads=[pv], writes=[sq.all()])

                def ssq_mm():
                    S.op("pe", lambda e: e.matmul(pss.ap, lhsT=onesB.ap, rhs=sq.ap, start=(f == 0), stop=(f == 15)),
                         reads=[onesB, sq.all()], writes=[pss], signal=True)
                deferred.append(ssq_mm)

            def first_end(T):
                flush_deferred()
                rs = rstdb[T]
                S.op("dve", lambda e: e.tensor_scalar(out=rs.ap, in0=pss.ap, scalar1=1.0 / D, scalar2=EPS, op0=ALU.mult, op1=ALU.add),
                     reads=[pss], writes=[rs.all()])
                S.op("act", lambda e: e.activation(out=rs.ap, in_=rs.ap, func=AF.Sqrt), reads=[rs.all()], writes=[rs.all()])
                S.op("dve", lambda e: e.reciprocal(out=rs.ap, in_=rs.ap), reads=[rs.all()], writes=[rs.all()])

            def xload(T):
                row0 = r0 + HALO + 512 * T
                for j in range(4):
                    S.dma("sp", outb.ap[:, j, :], xin[row0 + 128 * j:row0 + 128 * j + 128, :], writes=[outb.v(j)])

            def stt(T, f):
                ot = otmp[f % 2]
                yd = yTb[T].v(f)
                rs = rstdb[T]
                S.op("dve", lambda e: e.scalar_tensor_tensor(out=ot.ap, in0=yd.ap, scalar=modv.ap[:, 32 + f:33 + f],
                                                             in1=rs.ap, op0=ALU.mult, op1=ALU.mult),
                     reads=[yd, rs.all(), modv.all()], writes=[ot.all()])

            def rest(T, f):
                ot = otmp[f % 2]
                bank = S.bank(ALLB7)
                pb = S.ps(bank)
                for j in range(4):
                    S.op("pe", lambda e, j=j: e.transpose(out=pb.ap[:, j * 128:(j + 1) * 128],
                                                          in_=ot.ap[:, j * 128:(j + 1) * 128], identity=identF.ap),
                         reads=[ot.all(), identF], writes=[pb], signal=(j == 3))
                od = outb.v(None, (f * 128, f * 128 + 128))
                S.op("dve", lambda e: e.tensor_tensor(out=od.ap, in0=od.ap,
                                                      in1=pb.ap.rearrange("p (j c) -> p j c", j=4), op=ALU.add),
                     reads=[pb, od], writes=[od])

            def store(T):
                orow = run * CH + 512 * T
                for j in range(4):
                    out_toks.append(S.dma("sp", out_d[orow + 128 * j:orow + 128 * j + 128, :], outb.ap[:, j, :], reads=[outb.v(j)]))

            xload(0)
            for f in range(16):
                first_f(0, f)
            first_end(0)
            stt(0, 0)
            for f in range(16):
                first_f(1, f)
                if f + 1 < 16:
                    stt(0, f + 1)
                rest(0, f)
            first_end(1)
            store(0)
            xload(1)
            stt(1, 0)
            for f in range(16):
                if f + 1 < 16:
                    stt(1, f + 1)
                rest(1, f)
            store(1)

        emit_run(0)
        emit_run(1)
        assert wstate["next"] == len(wseq)
        S.wait_tokens("sp", out_toks)
        with nc.Block() as block:
            S.replay(block)
    return nc


def _blocks(w, kc):
    K, N = w.shape
    nb = N // 128
    a = w.reshape(kc, 128, nb, 128)
    return np.ascontiguousarray(a.transpose(2, 1, 0, 3)).reshape(nb, 128, kc * 128)


def _masks(hv):
    k = np.arange(128)[:, None]
    q = np.arange(128)[None, :]
    LT = (k <= q).astype(np.float32)
    UT = (k >= q).astype(np.float32)
    lu4 = np.concatenate([LT, UT, LT, UT], axis=1)
    lu4h1 = np.concatenate([LT, UT * hv, LT, UT], axis=1)
    lu4h2 = np.concatenate([LT, UT * hv, LT, UT * hv], axis=1)
    g2a = np.concatenate([UT[:, 0:64] * hv] * 4 + [LT[:, 0:64]] * 4, axis=1)
    g2b = np.concatenate([LT[:, 64:128]] * 4 + [UT[:, 0:64] * hv] * 4, axis=1)
    return lu4, lu4h1, lu4h2, g2a, g2b


_PROG = {}


def kernel(x, c, positions, g_pre, w_ada, b_ada, w_in, conv_w, w_attn_o, w_conv_o, w_o, g_post):
    x = np.asarray(x, np.float32)[0]
    pos = np.asarray(positions, np.int32)[0]
    f32 = np.float32
    inv_freq = (500000.0 ** (-(np.arange(0, 32, 2, dtype=np.float64)) / 32.0)).astype(np.float32)
    try:
        import jax.numpy as jnp
        inv_freq = np.asarray(500000.0 ** (-jnp.arange(0, 32, 2, dtype=jnp.float32) / 32), np.float32)
    except Exception:
        pass
    win_r = _blocks(np.asarray(w_in, f32)[0], KC)
    wao_r = _blocks(np.asarray(w_attn_o, f32)[0], 8)
    wco_r = _blocks(np.asarray(w_conv_o, f32)[0], KC)
    wo_r = _blocks(np.asarray(w_o, f32)[0], KC)
    wada_r = np.ascontiguousarray(np.asarray(w_ada, f32)[0].reshape(KC, 128, 3 * D))

    def colT(v, n):
        return np.asarray(v, f32).reshape(n, 128).T

    Pm = np.zeros((128, 32), f32)
    for m in range(32):
        Pm[(m + 16) % 32, m] = 1.0
    in_maps = []
    for core in range(NCORES):
        hv = 0.0 if core == 0 else 1.0
        xin = np.zeros((HALO + TOK, D), f32)
        p32 = np.zeros((HALO + TOK,), np.int32)
        s = core * TOK
        if core > 0:
            xin[:HALO] = x[s - HALO:s]
            p32[:HALO] = pos[s - HALO:s]
        xin[HALO:] = x[s:s + TOK]
        p32[HALO:] = pos[s:s + TOK]
        cstf = np.zeros((128, NCF), f32)
        cstf[:, CF_IDENT:CF_IDENT + 128] = np.eye(128, dtype=f32)
        cstf[:, CF_GPRE:CF_GPRE + 16] = colT(g_pre[0], 16)
        cstf[:, CF_GPOST:CF_GPOST + 16] = colT(g_post[0], 16)
        cstf[:, CF_BADA:CF_BADA + 48] = colT(b_ada[0], 48)
        cstf[:, CF_C:CF_C + 16] = colT(c[0], 16)
        cw = np.asarray(conv_w, f32)[0]
        cstf[:, CF_CONVW:CF_CONVW + 48] = cw.T.reshape(16, 128, 3).transpose(1, 0, 2).reshape(128, 48)
        cstf[0:32, CF_INVF] = np.tile(inv_freq, 2)
        cstf[0:16, CF_SGN] = -1.0
        cstf[16:32, CF_SGN] = 1.0
        cstf[:, CF_ONE] = 1.0
        cstf[:, CF_HV] = hv
        cstf[:, CF_EPS] = EPS
        cstb = np.zeros((128, NCB), f32)
        cstb[:, CB_IDENT:CB_IDENT + 128] = np.eye(128, dtype=f32)
        cstb[:, CB_ONES:CB_ONES + 128] = 1.0
        cstb[:, CB_PM:CB_PM + 32] = Pm
        lu4, lu4h1, lu4h2, g2a, g2b = _masks(hv)
        cstb[:, CB_LU4:CB_LU4 + 512] = lu4
        cstb[:, CB_LU4H1:CB_LU4H1 + 512] = lu4h1
        cstb[:, CB_LU4H2:CB_LU4H2 + 512] = lu4h2
        cstb[:, CB_G2A:CB_G2A + 512] = g2a
        cstb[:, CB_G2B:CB_G2B + 512] = g2b
        in_maps.append({
            "xin": xin, "pos32": np.ascontiguousarray(np.broadcast_to(p32[None, :], (32, HALO + TOK))),
            "cstf": cstf, "cstb": cstb, "wada": wada_r, "win": win_r, "wao": wao_r, "wco": wco_r, "wo": wo_r,
        })
    if "nc" not in _PROG:
        _PROG["nc"] = build_program()
    res = run_bass_kernel_spmd(_PROG["nc"], in_maps, core_ids=list(range(NCORES)))
    out = np.concatenate([np.asarray(r["out"], f32) for r in res.results], axis=0)
    return out.reshape(1, SEQ, D)
```

```python
from contextlib import ExitStack
import numpy as np
import concourse.bass as bass
import concourse.mybir as mybir
from concourse.bass_utils import run_bass_kernel_spmd

F32 = mybir.dt.float32
BF16 = mybir.dt.bfloat16
I32 = mybir.dt.int32
U8 = mybir.dt.uint8
AF = mybir.ActivationFunctionType
ALU = mybir.AluOpType
ESZ = {F32: 4, BF16: 2, I32: 4, U8: 1}

NCORES = 8
D = 2048
SEQ = 16384
TOK = SEQ // NCORES
CH = 1024
HALO = 2048
KC = D // 128
PROJ = 22528
EPS = 1e-6
CELL = 256
ENGS = ("pe", "act", "dve", "pool", "sp")
SBUF_BYTES = 207 * 1024


class View:
    __slots__ = ("ap", "space", "lo", "hi")

    def __init__(self, ap, space, lo, hi):
        self.ap, self.space, self.lo, self.hi = ap, space, lo, hi


class Buf:
    def __init__(self, S, lo, shape, dtype, parts=128):
        self.S, self.lo, self.shape, self.dtype, self.parts = S, lo, list(shape), dtype, parts
        self.esz = ESZ[dtype]
        self.n = int(np.prod(shape))
        ap = S.arena[0:parts, lo:lo + self.n * self.esz]
        if dtype != U8:
            ap = ap.bitcast(dtype)
        if len(shape) == 2:
            ap = ap.rearrange("p (a b) -> p a b", a=shape[0])
        elif len(shape) == 3:
            ap = ap.rearrange("p (a b c) -> p a b c", a=shape[0], b=shape[1])
        self.ap = ap
        self.hi = lo + self.n * self.esz

    def all(self):
        return View(self.ap, "sb", self.lo, self.hi)

    def v(self, *idx, p=None):
        strides = []
        s = 1
        for d in reversed(self.shape):
            strides.insert(0, s)
            s *= d
        key = [slice(None) if p is None else slice(p[0], p[1])]
        lo = 0
        hi = 0
        for i, d in enumerate(self.shape):
            ix = idx[i] if i < len(idx) else None
            if ix is None:
                key.append(slice(None))
                a, b = 0, d
            elif isinstance(ix, int):
                key.append(ix)
                a, b = ix, ix + 1
            else:
                a, b = ix[0], ix[1]
                st = ix[2] if len(ix) > 2 else 1
                key.append(slice(a, b, st) if st != 1 else slice(a, b))
            assert 0 <= a < b <= d, (idx, self.shape)
            lo += a * strides[i]
            hi += (b - 1) * strides[i]
        hi += 1
        return View(self.ap[tuple(key)], "sb", self.lo + lo * self.esz, self.lo + hi * self.esz)


class Sched:
    def __init__(self, nc, stack, sbuf_bytes, n_dma_sems=12):
        self.nc = nc
        self.ops = {e: [] for e in ENGS}
        self.sems = {}
        for e in ENGS:
            self.sems[e] = stack.enter_context(nc.semaphore("s_" + e))
        self.cnt = {e: 0 for e in ENGS}
        self.pending = {e: False for e in ENGS}
        self.known = {e: {} for e in ENGS}
        self.ndma = n_dma_sems
        self.dma_val = {}
        self.dma_rr = {}
        for q in ("sp", "pool", "act"):
            for i in range(n_dma_sems):
                self.sems[("d", q, i)] = stack.enter_context(nc.semaphore("s_dma_%s%d" % (q, i)))
                self.dma_val[(q, i)] = 0
            self.dma_rr[q] = 0
        ncell = (sbuf_bytes + CELL - 1) // CELL
        self.cw = {"sb": [None] * ncell, "ps": [None] * 8}
        self.cr = {"sb": [dict() for _ in range(ncell)], "ps": [dict() for _ in range(8)]}
        self.arena = stack.enter_context(nc.sbuf_tensor("arena", [128, sbuf_bytes], U8))
        self.psum = [stack.enter_context(nc.psum_tensor("pb%d" % i, [128, 512], F32)) for i in range(8)]
        self.sbuf_bytes = sbuf_bytes
        self.rr = {}

    def buf(self, lo, shape, dtype, parts=128):
        b = Buf(self, lo, shape, dtype, parts)
        assert b.hi <= self.sbuf_bytes, "arena overflow"
        return b

    def ps(self, bank, dtype=F32, parts=128, cols=None):
        t = self.psum[bank]
        ap = t[0:parts, :]
        if dtype != F32:
            ap = ap.bitcast(dtype)
        if cols is not None:
            ap = ap[:, cols[0]:cols[1]]
        return View(ap, "ps", bank, bank + 1)

    def bank(self, pool):
        i = self.rr.get(pool, 0)
        self.rr[pool] = (i + 1) % len(pool)
        return pool[i]

    def _cells(self, v):
        if v.space == "ps":
            return range(v.lo, v.hi)
        return range(v.lo // CELL, (v.hi + CELL - 1) // CELL)

    def _deps(self, eng, reads, writes):
        need = {}

        def add(tok):
            if tok is None:
                return
            k, val = tok
            if k == "pe" and eng == "pe":
                return
            if need.get(k, 0) < val:
                need[k] = val

        for v in reads:
            cw, cr = self.cw[v.space], self.cr[v.space]
            for c in self._cells(v):
                add(cw[c])
                if v.space == "ps":
                    for k, val in cr[c].items():
                        if k != eng:
                            add((k, val))
        for v in writes:
            cw, cr = self.cw[v.space], self.cr[v.space]
            for c in self._cells(v):
                add(cw[c])
                for k, val in cr[c].items():
                    add((k, val))
        kn = self.known[eng]
        waits = []
        for k, val in need.items():
            if kn.get(k, 0) >= val:
                continue
            kn[k] = val
            waits.append((k, val))
        return waits

    def _record(self, tok, reads, writes):
        k, val = tok
        for v in reads:
            cr = self.cr[v.space]
            for c in self._cells(v):
                cr[c][k] = val
        for v in writes:
            cw, cr = self.cw[v.space], self.cr[v.space]
            for c in self._cells(v):
                cw[c] = tok
                cr[c] = {}

    def op(self, eng, fn, reads=(), writes=(), signal=True):
        waits = self._deps(eng, reads, writes)
        tok = (eng, self.cnt[eng] + 1)
        if signal:
            self.cnt[eng] += 1
            self.pending[eng] = False
        else:
            self.pending[eng] = True
        self._record(tok, reads, writes)
        self.ops[eng].append((waits, fn, ("inc", eng) if signal else None))
        return tok

    def dma(self, queue, out, in_, reads=(), writes=(), **kw):
        i = self.dma_rr[queue]
        self.dma_rr[queue] = (i + 1) % self.ndma
        waits = self._deps(queue, reads, writes)
        kn = self.known[queue]
        sk = ("d", queue, i)
        if self.dma_val[(queue, i)] > 0 and kn.get(sk, 0) < self.dma_val[(queue, i)]:
            kn[sk] = self.dma_val[(queue, i)]
            waits.append((sk, self.dma_val[(queue, i)]))
        self.dma_val[(queue, i)] += 16
        tok = (sk, self.dma_val[(queue, i)])
        self._record(tok, reads, writes)

        def fn(e, out=out, in_=in_, kw=kw):
            return e.dma_start(out=out, in_=in_, **kw)

        self.ops[queue].append((waits, fn, ("dma", sk)))
        return tok

    def wait_tokens(self, eng, toks):
        kn = self.known[eng]
        waits = []
        for k, val in toks:
            if kn.get(k, 0) < val:
                kn[k] = val
                waits.append((k, val))
        if waits:
            self.ops[eng].append((waits, None, None))

    def replay(self, block):
        sems = self.sems

        def run(e, lst):
            for waits, fn, post in lst:
                for k, val in waits:
                    e.wait_ge(sems[k], val)
                if fn is None:
                    continue
                ins = fn(e)
                if post is None:
                    continue
                if post[0] == "inc":
                    ins.then_inc(sems[post[1]], 1)
                else:
                    ins.then_inc(sems[post[1]], 16)

        for e in ENGS:
            assert not self.pending[e], "engine %s ends with unsignaled instruction" % e

        @block.tensor
        def _(e):
            run(e, self.ops["pe"])

        @block.scalar
        def _(e):
            run(e, self.ops["act"])

        @block.vector
        def _(e):
            run(e, self.ops["dve"])

        @block.gpsimd
        def _(e):
            run(e, self.ops["pool"])

        @block.sync
        def _(e):
            run(e, self.ops["sp"])


CF_IDENT = 0
CF_GPRE = 128
CF_GPOST = 144
CF_BADA = 160
CF_C = 208
CF_CONVW = 224
CF_INVF = 272
CF_SGN = 273
CF_ONE = 274
CF_HV = 275
CF_EPS = 276
NCF = 277
CB_IDENT = 0
CB_ONES = 128
CB_PM = 256
CB_LU4 = 288
CB_LU4H1 = 800
CB_LU4H2 = 1312
CB_G2A = 1824
CB_G2B = 2336
NCB = 2848

GEN = (0, 1, 2, 3)
OB = (4, 5)
DB = (6, 7)
ALLB = (0, 1, 2, 3, 4, 5, 6, 7)
ALLB7 = (0, 1, 2, 3, 4, 5, 6)


def build_program():
    nc = bass.Bass("TRN2", target_bir_lowering=False)
    xin = nc.dram_tensor("xin", [HALO + TOK, D], F32, kind="ExternalInput").ap()
    pos_d = nc.dram_tensor("pos32", [32, HALO + TOK], I32, kind="ExternalInput").ap()
    cf_d = nc.dram_tensor("cstf", [128, NCF], F32, kind="ExternalInput").ap()
    cb_d = nc.dram_tensor("cstb", [128, NCB], F32, kind="ExternalInput").ap()
    wada_d = nc.dram_tensor("wada", [KC, 128, 3 * D], F32, kind="ExternalInput").ap()
    win_d = nc.dram_tensor("win", [PROJ // 128, 128, KC * 128], F32, kind="ExternalInput").ap()
    wao_d = nc.dram_tensor("wao", [16, 128, 8 * 128], F32, kind="ExternalInput").ap()
    wco_d = nc.dram_tensor("wco", [16, 128, KC * 128], F32, kind="ExternalInput").ap()
    wo_d = nc.dram_tensor("wo", [16, 128, KC * 128], F32, kind="ExternalInput").ap()
    out_d = nc.dram_tensor("out", [TOK, D], F32, kind="ExternalOutput").ap()

    with ExitStack() as st:
        S = Sched(nc, st, SBUF_BYTES)
        top = [0]

        def alloc(nbytes):
            lo = (top[0] + CELL - 1) // CELL * CELL
            top[0] = lo + nbytes
            assert top[0] <= SBUF_BYTES, "SBUF overflow %d" % top[0]
            return lo

        def newbuf(shape, dtype, parts=128):
            return S.buf(alloc(int(np.prod(shape)) * ESZ[dtype]), shape, dtype, parts)

        cf = newbuf([NCF], F32)
        cb = newbuf([NCB], BF16)
        modv = newbuf([64], F32)
        modraw = newbuf([64], F32)
        NW = 3
        wbufs = [newbuf([KC, 128], BF16) for _ in range(NW)]
        hT_own = newbuf([KC, CH], BF16)
        aT = newbuf([8, CH], BF16)
        hT_h2 = newbuf([KC, 2], BF16)
        X0 = alloc(0)
        XBYTES = SBUF_BYTES - X0
        assert XBYTES >= 130 * 1024, XBYTES

        def xbuf(off, shape, dtype, parts=128):
            b = S.buf(X0 + off, shape, dtype, parts)
            assert b.hi <= SBUF_BYTES
            return b

        hT_halo = xbuf(0, [KC, HALO], BF16)
        A0 = 64 * 1024
        cosT = xbuf(A0, [CH + HALO], F32, parts=32)
        sinT = xbuf(A0 + 12288, [CH + HALO], F32, parts=32)
        A1 = A0 + 24576
        qT = xbuf(A1, [CH], BF16)
        kT = xbuf(A1 + 2048, [CH + HALO], BF16)
        vT = xbuf(A1 + 8192, [CH + HALO], BF16)
        Vt = xbuf(A1 + 14336, [32, 128], BF16)
        szb = xbuf(A1 + 22528, [CH], F32)
        numb = xbuf(A1 + 26624, [CH], F32)
        denb = xbuf(A1 + 30720, [CH], F32)
        pTb = [xbuf(A1 + 34816 + i * 1024, [512], BF16) for i in range(4)]
        rtmp = [xbuf(A1 + 38912 + i * 2048, [512], F32, parts=32) for i in range(2)]
        A_END = A1 + 43008
        assert A_END <= XBYTES, (A_END, XBYTES)
        xts = [xbuf(A1, [D], F32), S.buf(aT.lo, [D], F32), S.buf(aT.lo + 8192, [D], F32)]
        xh8 = [xbuf(A1 + 8192 + i * 4096, [D], BF16) for i in range(8)]
        stat8 = [xbuf(A1 + 40960 + i * 256, [8], F32) for i in range(8)]
        assert A1 + 40960 + 8 * 256 <= A_END
        rp_k = xbuf(A_END, [1536], F32, parts=32)
        assert A_END + 6144 <= XBYTES
        wa = [xbuf(A0 + i * 2048, [512], F32) for i in range(8)] + [xbuf(A_END + i * 2048, [512], F32) for i in range(4)]
        acc = xbuf(A0 + 16384, [2048], F32)
        assert A_END + 4 * 2048 <= XBYTES
        wa_g = [xbuf(98304 + i * 2048, [512], F32) for i in range(4)]
        acc_g = xbuf(98304 + 8192, [2048], F32)
        gcT = xbuf(0, [KC, CH], BF16)
        mT = xbuf(32768, [KC, CH], BF16)
        yT = xbuf(65536, [KC, 512], F32)
        yTb = [yT, xbuf(0, [KC, 512], F32)]
        outb = xbuf(98304, [4, D], F32)
        ccs = [xbuf(65536 + i * 2048, [512], F32) for i in range(2)]
        pbuf = xbuf(65536 + 4096, [CH + 2], F32)
        ubuf = xbuf(65536 + 4096 + 4352, [CH], F32)
        szc = [xbuf(65536 + 12544 + i * 2048, [512], F32) for i in range(2)]
        cch = xbuf(65536 + 16640, [2], F32)
        sg = [xbuf(65536 + 16896 + i * 2048, [512], F32) for i in range(4)]
        rstd = xbuf(98304 + 32768, [512], F32)
        o0 = 98304 + 32768 + 2048
        sqb = [xbuf(o0 + i * 1024, [512], BF16) for i in range(2)]
        otmp = [xbuf(o0 + 2048 + i * 2048, [512], F32) for i in range(2)]
        rstdb = [rstd, xbuf(o0 + 6144, [512], F32)]
        assert o0 + 8192 <= XBYTES, (o0, XBYTES)
        assert o0 + 2048 + 4096 <= XBYTES, (o0, XBYTES)

        identF = cf.v((CF_IDENT, CF_IDENT + 128))
        identB = cb.v((CB_IDENT, CB_IDENT + 128))
        onesB = cb.v((CB_ONES, CB_ONES + 128))

        def cfcol(c0, n=1, p=None):
            return cf.v((c0, c0 + n), p=p)

        S.dma("sp", cf.ap, cf_d, writes=[cf.all()])
        S.dma("pool", cb.ap[:, 0:1424], cb_d[:, 0:1424], writes=[cb.v((0, 1424))])
        S.dma("pool", cb.ap[:, 1424:NCB], cb_d[:, 1424:NCB], writes=[cb.v((1424, NCB))])

        mod_steps = []
        gate_steps = []

        def mk_third(j, wabufs, accb, queue, steps):
            cnt = [0]
            for kc in range(KC):
                for pc in range(4):
                    def step(kc=kc, pc=pc):
                        w = wabufs[cnt[0] % len(wabufs)]
                        cnt[0] += 1
                        col = j * 2048 + pc * 512
                        S.dma(queue, w.ap, wada_d[kc][:, col:col + 512], writes=[w.all()])
                        av = accb.v((pc * 512, pc * 512 + 512))
                        if kc == 0:
                            S.op("dve", lambda e: e.tensor_scalar(out=av.ap, in0=w.ap, scalar1=cf.ap[:, CF_C:CF_C + 1],
                                                                  scalar2=None, op0=ALU.mult),
                                 reads=[w.all(), cf.all()], writes=[av])
                        else:
                            S.op("dve", lambda e: e.scalar_tensor_tensor(
                                out=av.ap, in0=w.ap, scalar=cf.ap[:, CF_C + kc:CF_C + kc + 1], in1=av.ap,
                                op0=ALU.mult, op1=ALU.add), reads=[w.all(), av, cf.all()], writes=[av])
                    steps.append(step)

            def fin_third():
                pm = S.ps(S.bank(ALLB))
                for i in range(16):
                    S.op("pe", lambda e, i=i: e.matmul(pm.ap[:, i:i + 1], lhsT=accb.ap[:, 128 * i:128 * i + 128],
                                                       rhs=cf.ap[:, CF_ONE:CF_ONE + 1], start=True, stop=True),
                         reads=[accb.all(), cf.all()], writes=[pm], signal=(i == 15))
                b0 = CF_BADA + 16 * j
                mr = modraw.v((16 * j, 16 * j + 16))
                S.op("dve", lambda e: e.tensor_tensor(out=mr.ap, in0=pm.ap[:, 0:16], in1=cf.ap[:, b0:b0 + 16], op=ALU.add),
                     reads=[pm, cf.all()], writes=[modraw.all()])
            steps.append(fin_third)

        mk_third(0, wa, acc, "pool", mod_steps)
        mk_third(1, wa, acc, "pool", mod_steps)

        def fin_gs():
            S.op("dve", lambda e: e.scalar_tensor_tensor(out=modv.ap[:, 0:16], in0=modraw.ap[:, 16:32], scalar=1.0,
                                                         in1=cf.ap[:, CF_GPRE:CF_GPRE + 16], op0=ALU.add, op1=ALU.mult),
                 reads=[modraw.all(), cf.all()], writes=[modv.all()])
            S.op("dve", lambda e: e.tensor_copy(out=modv.ap[:, 16:32], in_=modraw.ap[:, 0:16]),
                 reads=[modraw.all()], writes=[modv.all()])
        mod_steps.append(fin_gs)

        mk_third(2, wa_g, acc_g, "sp", gate_steps)

        def fin_gg():
            S.op("dve", lambda e: e.tensor_tensor(out=modv.ap[:, 32:48], in0=modraw.ap[:, 32:48],
                                                  in1=cf.ap[:, CF_GPOST:CF_GPOST + 16], op=ALU.mult),
                 reads=[modraw.all(), cf.all()], writes=[modv.all()])
        gate_steps.append(fin_gg)

        def gate_slot(n):
            for _ in range(n):
                if gate_steps:
                    gate_steps.pop(0)()

        def mod_slot(n=4):
            for _ in range(n):
                if mod_steps:
                    mod_steps.pop(0)()

        wseq = []
        for run in range(2):
            for h in range(8):
                for g in range(3):
                    wseq += [("in", g * 8 + h), ("in", 24 + g * 8 + h), ("in", 48 + g * 8 + h)]
                wseq.append(("in", 72 + h))
            for c in range(16):
                wseq += [("in", 96 + c), ("in", 112 + c), ("in", 80 + c), ("in", 128 + c)]
            for f in range(16):
                wseq += [("ao", f), ("in", 144 + f), ("co", f), ("in", 160 + f)]
            for T in range(2):
                for f in range(16):
                    wseq.append(("o", f))
        wstate = {"issued": 0, "next": 0}

        def w_issue(j):
            kind, idx = wseq[j]
            b = wbufs[j % NW]
            if kind == "ao":
                S.dma("pool", b.ap[:, 0:8, :], wao_d[idx].rearrange("p (a b) -> p a b", a=8), writes=[b.all()])
            else:
                src = {"in": win_d, "co": wco_d, "o": wo_d}[kind][idx]
                S.dma("pool", b.ap, src.rearrange("p (a b) -> p a b", a=KC), writes=[b.all()])

        def w_get(expect):
            i = wstate["next"]
            assert wseq[i] == expect, (i, wseq[i], expect)
            wstate["next"] = i + 1
            while wstate["issued"] < min(len(wseq), i + NW):
                w_issue(wstate["issued"])
                wstate["issued"] += 1
            return wbufs[i % NW]

        deferred = []

        def flush_deferred():
            lst = deferred[:]
            del deferred[:]
            for fn in lst:
                fn()

        def hT_src(c0, c1):
            if c1 <= HALO:
                return lambda kc: hT_halo.v(kc, (c0, c1))
            assert c0 >= HALO
            return lambda kc: hT_own.v(kc, (c0 - HALO, c1 - HALO))

        def proj(wb, c0, c1, nk=KC, src=None, pool=GEN):
            n = c1 - c0
            bank = S.bank(pool)
            pv = S.ps(bank, cols=(0, n))
            srcf = src if src is not None else hT_src(c0, c1)
            for kc in range(nk):
                r = srcf(kc)
                S.op("pe", lambda e, kc=kc, r=r: e.matmul(pv.ap, lhsT=wb.ap[:, kc, :], rhs=r.ap,
                                                          start=(kc == 0), stop=(kc == nk - 1)),
                     reads=[wb.v(kc), r], writes=[pv], signal=(kc == nk - 1))
            flush_deferred()
            return pv

        def tiles(c0, c1):
            res = []
            c = c0
            while c < c1:
                lim = HALO if c < HALO else c1
                e_ = min(c + 512, lim, c1)
                res.append((c, e_))
                c = e_
            return res

        out_toks = []

        def emit_run(run):
            r0 = run * CH
            NT = CH + HALO

            def rope_tables():
                inv2pi = float(np.float32(1.0 / (2.0 * np.pi)))
                MAGIC = 12582912.0
                C1 = 6.28125
                C2 = float(np.float32(2.0 * np.pi - 6.28125))
                PI_LO = 3.1415925
                def half(hf):
                    a, b = hf * 1536, hf * 1536 + 1536
                    sv = sinT.v((a, b))
                    cv = cosT.v((a, b))
                    kv_ = rp_k.all()
                    S.dma("pool", sv.ap, pos_d[:, r0 + a:r0 + b], writes=[sv])
                    S.op("dve", lambda e: e.tensor_scalar(out=sv.ap, in0=sv.ap, scalar1=cf.ap[0:32, CF_INVF:CF_INVF + 1],
                                                           scalar2=None, op0=ALU.mult), reads=[sv, cf.all()], writes=[sv])
                    S.op("dve", lambda e: e.tensor_scalar(out=kv_.ap, in0=sv.ap, scalar1=inv2pi, scalar2=MAGIC,
                                                           op0=ALU.mult, op1=ALU.add), reads=[sv], writes=[kv_])
                    S.op("dve", lambda e: e.tensor_scalar(out=kv_.ap, in0=kv_.ap, scalar1=MAGIC, scalar2=None,
                                                           op0=ALU.subtract), reads=[kv_], writes=[kv_])
                    for Cc in (C1, C2):
                        S.op("dve", lambda e, Cc=Cc: e.tensor_scalar(out=cv.ap, in0=kv_.ap, scalar1=-Cc, scalar2=None, op0=ALU.mult),
                             reads=[kv_], writes=[cv])
                        S.op("dve", lambda e: e.tensor_tensor(out=sv.ap, in0=sv.ap, in1=cv.ap, op=ALU.add),
                             reads=[sv, cv], writes=[sv])
                    S.op("dve", lambda e: e.tensor_scalar(out=sv.ap, in0=sv.ap, scalar1=PI_LO, scalar2=-PI_LO,
                                                          op0=ALU.min, op1=ALU.max), reads=[sv], writes=[sv])
                half(0)
                half(1)

            def rope_tables_act():
                def half(hf):
                    a, b = hf * 1536, hf * 1536 + 1536
                    sv = sinT.v((a, b))
                    cv = cosT.v((a, b))
                    S.op("act", lambda e: e.activation(out=cv.ap, in_=sv.ap, func=AF.Sin, scale=0.5), reads=[sv], writes=[cv])
                    S.op("act", lambda e: e.activation(out=sv.ap, in_=sv.ap, func=AF.Sin, scale=cf.ap[0:32, CF_SGN:CF_SGN + 1]),
                         reads=[sv, cf.all()], writes=[sv])
                    S.op("dve", lambda e: e.tensor_tensor(out=cv.ap, in0=cv.ap, in1=cv.ap, op=ALU.mult), reads=[cv], writes=[cv])
                    S.op("dve", lambda e: e.tensor_scalar(out=cv.ap, in0=cv.ap, scalar1=-2.0, scalar2=1.0,
                                                           op0=ALU.mult, op1=ALU.add), reads=[cv], writes=[cv])
                half(0)
                half(1)


            tile_ctr = [0]
            rope_done = [False]

            def mod_slot0():
                mod_slot()
                if not mod_steps and not rope_done[0]:
                    rope_done[0] = True
                    rope_tables()

            def stage1(G):
                def fin(i, x_, st):
                    S.op("act", lambda e: e.activation(out=st.ap[:, 2:3], in_=st.ap[:, 0:1], func=AF.Sqrt, scale=1.0 / D, bias=cf.ap[:, CF_EPS:CF_EPS + 1]),
                         reads=[st.all(), cf.all()], writes=[st.all()])
                    S.op("dve", lambda e: e.reciprocal(out=st.ap[:, 3:4], in_=st.ap[:, 2:3]),
                         reads=[st.all()], writes=[st.all()])
                    xh_ = xh8[(G % 2) * 4 + i]
                    S.op("dve", lambda e: e.tensor_scalar(out=xh_.ap, in0=x_.ap, scalar1=st.ap[:, 3:4],
                                                          scalar2=None, op0=ALU.mult),
                         reads=[x_.all(), st.all()], writes=[xh_.all()])
                prevt = None
                for i in range(4):
                    row = r0 + G * 512 + i * 128
                    x_ = xts[tile_ctr[0] % 3]
                    tile_ctr[0] += 1
                    st = stat8[(G % 2) * 4 + i]
                    xh_ = xh8[(G % 2) * 4 + i]
                    S.dma("sp", x_.ap, xin[row:row + 128, :], writes=[x_.all()])
                    S.op("act", lambda e, x_=x_, st=st, xh_=xh_: e.activation(out=xh_.ap, in_=x_.ap, func=AF.Square,
                                                                           accum_out=st.ap[:, 0:1]),
                         reads=[x_.all()], writes=[xh_.all(), st.all()])
                    if prevt is not None:
                        fin(*prevt)
                    prevt = (i, x_, st)
                    if run == 0:
                        mod_slot0()
                fin(*prevt)

            def stage2(G):
                for kc in range(KC):
                    bank = S.bank(ALLB)
                    pb = S.ps(bank, BF16, cols=(0, 512))
                    for i in range(4):
                        xh_ = xh8[(G % 2) * 4 + i]
                        S.op("pe", lambda e, i=i, kc=kc, pb=pb, xh_=xh_: e.transpose(out=pb.ap[:, i * 128:(i + 1) * 128],
                                                                                   in_=xh_.ap[:, kc * 128:(kc + 1) * 128],
                                                                                   identity=identB.ap),
                             reads=[xh_.v((kc * 128, kc * 128 + 128)), identB], writes=[pb], signal=(i == 3))
                    c0 = G * 512
                    dst = hT_halo.v(kc, (c0, c0 + 512)) if c0 < HALO else hT_own.v(kc, (c0 - HALO, c0 - HALO + 512))
                    if run == 0:
                        if kc % 2 == 0:
                            S.op("act", lambda e, pb=pb, dst=dst: e.copy(out=dst.ap, in_=pb.ap), reads=[pb], writes=[dst])
                        else:
                            S.op("dve", lambda e, pb=pb, dst=dst: e.tensor_copy(out=dst.ap, in_=pb.ap), reads=[pb], writes=[dst])
                    else:
                        if kc % 2 == 0:
                            S.op("act", lambda e, pb=pb, dst=dst, kc=kc: e.activation(
                                out=dst.ap, in_=pb.ap, func=AF.Identity, scale=modv.ap[:, kc:kc + 1], bias=modv.ap[:, 16 + kc:17 + kc]),
                                reads=[pb, modv.all()], writes=[dst])
                        else:
                            S.op("dve", lambda e, pb=pb, dst=dst, kc=kc: e.tensor_scalar(
                                out=dst.ap, in0=pb.ap, scalar1=modv.ap[:, kc:kc + 1], scalar2=modv.ap[:, 16 + kc:17 + kc],
                                op0=ALU.mult, op1=ALU.add), reads=[pb, modv.all()], writes=[dst])
                    if run == 0 and kc % 4 == 3:
                        mod_slot0()

            NG = NT // 512
            if run == 0:
                glist = list(range(NG))
            else:
                glist = [0, 1, 4, 5]
                for kc in range(KC):
                    src_ = hT_own.v(kc)
                    dst_ = hT_halo.v(kc, (1024, 2048))
                    if kc % 2 == 0:
                        S.op("act", lambda e, src_=src_, dst_=dst_: e.copy(out=dst_.ap, in_=src_.ap), reads=[src_], writes=[dst_])
                    else:
                        S.op("dve", lambda e, src_=src_, dst_=dst_: e.tensor_copy(out=dst_.ap, in_=src_.ap), reads=[src_], writes=[dst_])
            stage1(glist[0])
            for gi, G in enumerate(glist):
                if gi + 1 < len(glist):
                    stage1(glist[gi + 1])
                stage2(G)
                if run == 1 and gi == 1:
                    rope_tables()
            if run == 0:
                while not rope_done[0]:
                    mod_slot0()
                for buf_ in (hT_own, hT_halo):
                    for kc in range(KC):
                        v_ = buf_.v(kc)
                        if kc % 2 == 0:
                            S.op("act", lambda e, v_=v_, kc=kc: e.activation(
                                out=v_.ap, in_=v_.ap, func=AF.Identity, scale=modv.ap[:, kc:kc + 1], bias=modv.ap[:, 16 + kc:17 + kc]),
                                reads=[v_, modv.all()], writes=[v_])
                        else:
                            S.op("dve", lambda e, v_=v_, kc=kc: e.tensor_scalar(
                                out=v_.ap, in0=v_.ap, scalar1=modv.ap[:, kc:kc + 1], scalar2=modv.ap[:, 16 + kc:17 + kc],
                                op0=ALU.mult, op1=ALU.add), reads=[v_, modv.all()], writes=[v_])
            rope_tables_act()

            S.op("dve", lambda e: e.tensor_copy(out=hT_h2.ap, in_=hT_halo.ap[:, :, HALO - 2:HALO]),
                 reads=[hT_halo.all()], writes=[hT_h2.all()])

            def rope_evac(pv, dstbuf, d0, n, tabc0):
                dst = dstbuf.v((d0, d0 + n))
                S.op("act", lambda e: e.copy(out=dst.ap, in_=pv.ap), reads=[pv], writes=[dst])
                t1 = rtmp[0]
                t2 = rtmp[1]
                pv32 = View(pv.ap[0:32, :], "ps", pv.lo, pv.hi)
                S.op("dve", lambda e: e.tensor_tensor(out=t1.ap[:, 0:n], in0=pv32.ap, in1=cosT.ap[:, tabc0:tabc0 + n], op=ALU.mult),
                     reads=[pv, cosT.v((tabc0, tabc0 + n))], writes=[t1.v((0, n))])

                def post():
                    bank = S.bank(GEN)
                    pr = S.ps(bank, parts=32, cols=(0, n))
                    d32 = dstbuf.v((d0, d0 + n), p=(0, 32))
                    S.op("pe", lambda e: e.matmul(pr.ap, lhsT=cb.ap[0:32, CB_PM:CB_PM + 32], rhs=d32.ap, start=True, stop=True),
                         reads=[d32, cb.all()], writes=[pr])
                    S.op("dve", lambda e: e.tensor_tensor(out=t2.ap[:, 0:n], in0=pr.ap, in1=sinT.ap[:, tabc0:tabc0 + n], op=ALU.mult),
                         reads=[pr, sinT.v((tabc0, tabc0 + n))], writes=[t2.v((0, n))])
                    S.op("dve", lambda e: e.tensor_tensor(out=d32.ap, in0=t1.ap[:, 0:n], in1=t2.ap[:, 0:n], op=ALU.add),
                         reads=[t1.v((0, n)), t2.v((0, n))], writes=[d32])
                deferred.append(post)

            scale_qk = float(128 ** -0.5)
            for h in range(8):
                for g in range(3):
                    Dg = (1, 4, 16)[g]
                    halo_g = (128, 512, 2048)[g]
                    kc0 = HALO - halo_g
                    wb = w_get(("in", g * 8 + h))
                    for (c0, c1) in tiles(HALO, HALO + CH):
                        pv = proj(wb, c0, c1)
                        rope_evac(pv, qT, c0 - HALO, c1 - c0, c0)
                    wb = w_get(("in", 24 + g * 8 + h))
                    for (c0, c1) in tiles(kc0, HALO + CH):
                        pv = proj(wb, c0, c1)
                        rope_evac(pv, kT, c0, c1 - c0, c0)
                    wb = w_get(("in", 48 + g * 8 + h))
                    for (c0, c1) in tiles(kc0, HALO + CH):
                        pv = proj(wb, c0, c1)
                        dst = vT.v((c0, c1))
                        S.op("act", lambda e, dst=dst, pv=pv: e.copy(out=dst.ap, in_=pv.ap), reads=[pv], writes=[dst])
                    ktiles = []
                    pairs = []
                    if g == 0:
                        for j in range(-1, 8):
                            ktiles.append((HALO + 128 * j, 1, 128))
                        for n_ in range(8):
                            pairs.append((128 * n_, 1, 128, 128 * n_, n_ + 1, n_))
                        masks = [CB_LU4H1 if (run == 0 and b == 0) else CB_LU4 for b in range(4)]
                    elif g == 1:
                        for n_ in range(-1, 2):
                            for r in range(4):
                                ktiles.append((HALO + 512 * n_ + r, 4, 128))
                        for n_ in range(2):
                            for r in range(4):
                                pairs.append((512 * n_ + r, 4, 128, (n_ * 4 + r) * 128, (n_ + 1) * 4 + r, n_ * 4 + r))
                        masks = [CB_LU4H2 if (run == 0 and b < 2) else CB_LU4 for b in range(4)]
                    else:
                        if run == 0:
                            for r in range(16):
                                ktiles.append((r, 16, 128))
                            for r in range(16):
                                ktiles.append((HALO + r, 16, 64))
                            for r in range(16):
                                pairs.append((r, 16, 64, 64 * r, 16 + r, r))
                        else:
                            for r in range(16):
                                ktiles.append((1024 + r, 16, 128))
                            for r in range(16):
                                ktiles.append((r, 16, 64))
                            for r in range(16):
                                pairs.append((r, 16, 64, 64 * r, r, 16 + r))
                        masks = [CB_G2A if run == 0 else CB_G2B] * 4
                    flush_deferred()
                    nkt = len(ktiles)
                    for t0 in range(0, nkt, 8):
                        bank = S.bank(GEN)
                        pb = S.ps(bank, BF16)
                        tl = list(range(t0, min(nkt, t0 + 8)))
                        for t in tl:
                            cs, stp, nk_ = ktiles[t]
                            src = vT.v((cs, cs + (nk_ - 1) * stp + 1, stp))
                            S.op("pe", lambda e, t=t, t0=t0, nk_=nk_, src=src, pb=pb: e.transpose(
                                out=pb.ap[0:nk_, (t - t0) * 128:(t - t0 + 1) * 128], in_=src.ap, identity=identB.ap),
                                reads=[src, identB], writes=[pb], signal=(t == tl[-1]))
                        nkb = ktiles[tl[0]][2]
                        assert all(ktiles[t][2] == nkb for t in tl)
                        dst = Vt.v((t0, t0 + len(tl)), p=(0, nkb))
                        S.op("dve", lambda e, dst=dst, pb=pb, n_=len(tl), nkb=nkb: e.tensor_copy(
                            out=dst.ap, in_=pb.ap[0:nkb, 0:n_ * 128].rearrange("p (a b) -> p a b", a=n_)),
                            reads=[pb], writes=[dst])
                    nq = pairs[0][2]
                    ppb = 512 // (2 * nq)
                    nbk = len(pairs) // ppb
                    assert nbk == 4

                    def score_bank(b):
                        bank = S.bank(GEN)
                        pb = S.ps(bank)
                        for pi in range(ppb):
                            qs, qst, nq_, oc, tc, tp = pairs[b * ppb + pi]
                            qv = qT.v((qs, qs + (nq_ - 1) * qst + 1, qst))
                            for half, t in enumerate((tc, tp)):
                                cs, stp, nk_ = ktiles[t]
                                kv = kT.v((cs, cs + (nk_ - 1) * stp + 1, stp))
                                col = (pi * 2 + half) * nq_ if g < 2 else (pi * 64 if nk_ == 128 else 256 + pi * 64)
                                last = (pi == ppb - 1 and half == 1)
                                S.op("pe", lambda e, kv=kv, qv=qv, col=col, nk_=nk_, nq_=nq_, pb=pb: e.matmul(
                                    pb.ap[0:nk_, col:col + nq_], lhsT=kv.ap, rhs=qv.ap, start=True, stop=True),
                                    reads=[kv, qv], writes=[pb], signal=last)
                        pr = pTb[b % 4]
                        mo = masks[b]
                        segs = [(0, 128, 0, 512)] if g < 2 else [(0, 128, 0, 256), (0, 64, 256, 512)]
                        for (p0, p1, c0_, c1_) in segs:
                            prv = pr.v((c0_, c1_), p=(p0, p1))
                            S.op("act", lambda e, pb=pb, prv=prv, p0=p0, p1=p1, c0_=c0_, c1_=c1_: e.activation(
                                out=prv.ap, in_=pb.ap[p0:p1, c0_:c1_], func=AF.Exp, scale=scale_qk),
                                reads=[pb], writes=[prv])
                            S.op("dve", lambda e, prv=prv, mo=mo, p0=p0, p1=p1, c0_=c0_, c1_=c1_: e.tensor_tensor(
                                out=prv.ap, in0=prv.ap, in1=cb.ap[p0:p1, mo + c0_:mo + c1_], op=ALU.mult),
                                reads=[prv, cb.all()], writes=[prv])
                        return pr

                    def pv_bank(b, pm_):
                        for pi in range(ppb):
                            qs, qst, nq_, oc, tc, tp = pairs[b * ppb + pi]
                            ob = OB[oc // 512]
                            db = DB[oc // 512]
                            oc_ = oc % 512
                            ov = S.ps(ob, cols=(oc_, oc_ + nq_))
                            dv = S.ps(db, cols=(oc_, oc_ + nq_))
                            for half, t in enumerate((tc, tp)):
                                cs, stp, nk_ = ktiles[t]
                                col = (pi * 2 + half) * nq_ if g < 2 else (pi * 64 if nk_ == 128 else 256 + pi * 64)
                                rv = pm_.v((col, col + nq_), p=(0, nk_))
                                vv = Vt.v(t, p=(0, nk_))
                                S.op("pe", lambda e, ov=ov, vv=vv, rv=rv, half=half: e.matmul(
                                    ov.ap, lhsT=vv.ap, rhs=rv.ap, start=(half == 0), stop=(half == 1)),
                                    reads=[vv, rv], writes=[ov], signal=False)
                            for half, t in enumerate((tc, tp)):
                                cs, stp, nk_ = ktiles[t]
                                col = (pi * 2 + half) * nq_ if g < 2 else (pi * 64 if nk_ == 128 else 256 + pi * 64)
                                rv = pm_.v((col, col + nq_), p=(0, nk_))
                                S.op("pe", lambda e, dv=dv, rv=rv, half=half, nk_=nk_: e.matmul(
                                    dv.ap, lhsT=cb.ap[0:nk_, CB_ONES:CB_ONES + 128], rhs=rv.ap, start=(half == 0), stop=(half == 1)),
                                    reads=[cb.all(), rv], writes=[dv], signal=(half == 1))

                    pend = []
                    for b in range(nbk):
                        pend.append((b, score_bank(b)))
                        if len(pend) > 2:
                            pv_bank(*pend.pop(0))
                    while pend:
                        pv_bank(*pend.pop(0))
                    for bi in range(2):
                        ov = S.ps(OB[bi])
                        dv = S.ps(DB[bi])
                        if g == 0:
                            nv = numb.v((512 * bi, 512 * bi + 512))
                            dn = denb.v((512 * bi, 512 * bi + 512))
                            S.op("act", lambda e, nv=nv, ov=ov: e.copy(out=nv.ap, in_=ov.ap), reads=[ov], writes=[nv])
                            S.op("act", lambda e, dn=dn, dv=dv: e.copy(out=dn.ap, in_=dv.ap), reads=[dv], writes=[dn])
                        else:
                            if g == 1:
                                nap = numb.ap[:, 512 * bi:512 * bi + 512].rearrange("p (i r) -> p r i", r=4)
                                dap = denb.ap[:, 512 * bi:512 * bi + 512].rearrange("p (i r) -> p r i", r=4)
                                oap = ov.ap.rearrange("p (r i) -> p r i", r=4)
                                dvp = dv.ap.rearrange("p (r i) -> p r i", r=4)
                                nv = numb.v((512 * bi, 512 * bi + 512))
                                dn = denb.v((512 * bi, 512 * bi + 512))
                            else:
                                nap = numb.ap.rearrange("p (i r) -> p r i", r=16)[:, 8 * bi:8 * bi + 8, :]
                                dap = denb.ap.rearrange("p (i r) -> p r i", r=16)[:, 8 * bi:8 * bi + 8, :]
                                oap = ov.ap.rearrange("p (r i) -> p r i", r=8)
                                dvp = dv.ap.rearrange("p (r i) -> p r i", r=8)
                                nv = numb.all()
                                dn = denb.all()
                            S.op("dve", lambda e, nap=nap, oap=oap: e.tensor_tensor(out=nap, in0=nap, in1=oap, op=ALU.add),
                                 reads=[ov, nv], writes=[nv])
                            S.op("dve", lambda e, dap=dap, dvp=dvp: e.tensor_tensor(out=dap, in0=dap, in1=dvp, op=ALU.add),
                                 reads=[dv, dn], writes=[dn])
                wb = w_get(("in", 72 + h))
                for (c0, c1) in tiles(HALO, HALO + CH):
                    pv = proj(wb, c0, c1)
                    dst = szb.v((c0 - HALO, c1 - HALO))
                    S.op("act", lambda e, dst=dst, pv=pv: e.activation(out=dst.ap, in_=pv.ap, func=AF.Silu), reads=[pv], writes=[dst])
                S.op("dve", lambda e: e.reciprocal(out=denb.ap, in_=denb.ap), reads=[denb.all()], writes=[denb.all()])
                S.op("dve", lambda e: e.tensor_tensor(out=numb.ap, in0=numb.ap, in1=denb.ap, op=ALU.mult),
                     reads=[numb.all(), denb.all()], writes=[numb.all()])
                adst = aT.v(h)
                S.op("dve", lambda e, adst=adst: e.tensor_tensor(out=adst.ap, in0=numb.ap, in1=szb.ap, op=ALU.mult),
                     reads=[numb.all(), szb.all()], writes=[adst])

            for c in range(16):
                if run == 0:
                    gate_slot(5)
                wcc = w_get(("in", 96 + c))
                pvh = proj(wcc, HALO - 2, HALO, src=lambda kc: hT_h2.v(kc))
                S.op("act", lambda e, pvh=pvh: e.copy(out=cch.ap, in_=pvh.ap), reads=[pvh], writes=[cch.all()])
                pcc = []
                for T, (c0, c1) in enumerate(tiles(HALO, HALO + CH)):
                    pv = proj(wcc, c0, c1)
                    S.op("act", lambda e, pv=pv, T=T: e.copy(out=ccs[T].ap, in_=pv.ap), reads=[pv], writes=[ccs[T].all()])
                wcx = w_get(("in", 112 + c))
                pvh = proj(wcx, HALO - 2, HALO, src=lambda kc: hT_h2.v(kc))
                ph = pbuf.v((0, 2))
                S.op("dve", lambda e, pvh=pvh, ph=ph: e.tensor_tensor(out=ph.ap, in0=pvh.ap, in1=cch.ap, op=ALU.mult),
                     reads=[pvh, cch.all()], writes=[ph])
                if run == 0:
                    S.op("dve", lambda e, ph=ph: e.tensor_scalar(out=ph.ap, in0=ph.ap, scalar1=cf.ap[:, CF_HV:CF_HV + 1], scalar2=None,
                                                                 op0=ALU.mult), reads=[ph, cf.all()], writes=[ph])
                for T, (c0, c1) in enumerate(tiles(HALO, HALO + CH)):
                    pv = proj(wcx, c0, c1)
                    pp = pbuf.v((2 + 512 * T, 2 + 512 * T + 512))
                    S.op("dve", lambda e, pv=pv, pp=pp, T=T: e.tensor_tensor(out=pp.ap, in0=pv.ap, in1=ccs[T].ap, op=ALU.mult),
                         reads=[pv, ccs[T].all()], writes=[pp])
                cw0 = CF_CONVW + 3 * c
                S.op("dve", lambda e, cw0=cw0: e.tensor_scalar(out=ubuf.ap, in0=pbuf.ap[:, 2:CH + 2], scalar1=cf.ap[:, cw0 + 2:cw0 + 3],
                                                               scalar2=None, op0=ALU.mult),
                     reads=[pbuf.all(), cf.all()], writes=[ubuf.all()])
                S.op("dve", lambda e, cw0=cw0: e.scalar_tensor_tensor(out=ubuf.ap, in0=pbuf.ap[:, 1:CH + 1], scalar=cf.ap[:, cw0 + 1:cw0 + 2],
                                                                      in1=ubuf.ap, op0=ALU.mult, op1=ALU.add),
                     reads=[pbuf.all(), ubuf.all(), cf.all()], writes=[ubuf.all()])
                S.op("dve", lambda e, cw0=cw0: e.scalar_tensor_tensor(out=ubuf.ap, in0=pbuf.ap[:, 0:CH], scalar=cf.ap[:, cw0:cw0 + 1],
                                                                      in1=ubuf.ap, op0=ALU.mult, op1=ALU.add),
                     reads=[pbuf.all(), ubuf.all(), cf.all()], writes=[ubuf.all()])
                wcb = w_get(("in", 80 + c))
                for T, (c0, c1) in enumerate(tiles(HALO, HALO + CH)):
                    pv = proj(wcb, c0, c1)
                    uu = ubuf.v((512 * T, 512 * T + 512))
                    S.op("dve", lambda e, pv=pv, uu=uu: e.tensor_tensor(out=uu.ap, in0=pv.ap, in1=uu.ap, op=ALU.mult),
                         reads=[pv, uu], writes=[uu])
                wzc = w_get(("in", 128 + c))
                for T, (c0, c1) in enumerate(tiles(HALO, HALO + CH)):
                    pv = proj(wzc, c0, c1)
                    S.op("act", lambda e, pv=pv, T=T: e.activation(out=szc[T].ap, in_=pv.ap, func=AF.Silu),
                         reads=[pv], writes=[szc[T].all()])
                    uu = ubuf.v((512 * T, 512 * T + 512))
                    gd = gcT.v(c, (512 * T, 512 * T + 512))
                    S.op("dve", lambda e, uu=uu, gd=gd, T=T: e.tensor_tensor(out=gd.ap, in0=uu.ap, in1=szc[T].ap, op=ALU.mult),
                         reads=[uu, szc[T].all()], writes=[gd])

            if run == 0:
                gate_slot(1000)

            for f in range(16):
                wa_ = w_get(("ao", f))
                pya = [proj(wa_, HALO + 512 * T, HALO + 512 * T + 512, nk=8,
                            src=(lambda kc, T=T: aT.v(kc, (512 * T, 512 * T + 512))), pool=ALLB) for T in range(2)]
                wg = w_get(("in", 144 + f))
                for T in range(2):
                    pv = proj(wg, HALO + 512 * T, HALO + 512 * T + 512, pool=ALLB)
                    S.op("act", lambda e, pv=pv, T=T: e.activation(out=sg[T].ap, in_=pv.ap, func=AF.Sigmoid),
                         reads=[pv], writes=[sg[T].all()])
                    S.op("dve", lambda e, T=T, py=pya[T]: e.tensor_tensor(out=sg[T].ap, in0=py.ap, in1=sg[T].ap, op=ALU.mult),
                         reads=[pya[T], sg[T].all()], writes=[sg[T].all()])
                wc_ = w_get(("co", f))
                pyc = [proj(wc_, HALO + 512 * T, HALO + 512 * T + 512,
                            src=(lambda kc, T=T: gcT.v(kc, (512 * T, 512 * T + 512))), pool=ALLB) for T in range(2)]
                wg = w_get(("in", 160 + f))
                for T in range(2):
                    pv = proj(wg, HALO + 512 * T, HALO + 512 * T + 512, pool=ALLB)
                    S.op("act", lambda e, pv=pv, T=T: e.activation(out=sg[2 + T].ap, in_=pv.ap, func=AF.Sigmoid),
                         reads=[pv], writes=[sg[2 + T].all()])
                    S.op("dve", lambda e, T=T, py=pyc[T]: e.tensor_tensor(out=sg[2 + T].ap, in0=py.ap, in1=sg[2 + T].ap, op=ALU.mult),
                         reads=[pyc[T], sg[2 + T].all()], writes=[sg[2 + T].all()])
                    md = mT.v(f, (512 * T, 512 * T + 512))
                    S.op("dve", lambda e, T=T, md=md: e.tensor_tensor(out=md.ap, in0=sg[T].ap, in1=sg[2 + T].ap, op=ALU.add),
                         reads=[sg[T].all(), sg[2 + T].all()], writes=[md])

            pss = S.ps(7)

            def first_f(T, f):
                wo_ = w_get(("o", f))
                pv = proj(wo_, HALO + 512 * T, HALO + 512 * T + 512,
                          src=(lambda kc, T=T: mT.v(kc, (512 * T, 512 * T + 512))), pool=ALLB7)
                yd = yTb[T].v(f)
                S.op("act", lambda e: e.copy(out=yd.ap, in_=pv.ap), reads=[pv], writes=[yd])
                sq = sqb[f % 2]
                S.op("act", lambda e: e.activation(out=sq.ap, in_=pv.ap, func=AF.Square), reads=[pv], writes=[sq.all()])

                def ssq_mm():
                    S.op("pe", lambda e: e.matmul(pss.ap, lhsT=onesB.ap, rhs=sq.ap, start=(f == 0), stop=(f == 15)),
                         reads=[onesB, sq.all()], writes=[pss], signal=True)
                deferred.append(ssq_mm)

            def first_end(T):
                flush_deferred()
                rs = rstdb[T]
                S.op("dve", lambda e: e.tensor_scalar(out=rs.ap, in0=pss.ap, scalar1=1.0 / D, scalar2=EPS, op0=ALU.mult, op1=ALU.add),
                     reads=[pss], writes=[rs.all()])
                S.op("act", lambda e: e.activation(out=rs.ap, in_=rs.ap, func=AF.Sqrt), reads=[rs.all()], writes=[rs.all()])
                S.op("dve", lambda e: e.reciprocal(out=rs.ap, in_=rs.ap), reads=[rs.all()], writes=[rs.all()])

            def xload(T):
                row0 = r0 + HALO + 512 * T
                for j in range(4):
                    S.dma("sp", outb.ap[:, j, :], xin[row0 + 128 * j:row0 + 128 * j + 128, :], writes=[outb.v(j)])

            def stt(T, f):
                ot = otmp[f % 2]
                yd = yTb[T].v(f)
                rs = rstdb[T]
                S.op("dve", lambda e: e.scalar_tensor_tensor(out=ot.ap, in0=yd.ap, scalar=modv.ap[:, 32 + f:33 + f],
                                                             in1=rs.ap, op0=ALU.mult, op1=ALU.mult),
                     reads=[yd, rs.all(), modv.all()], writes=[ot.all()])

            def rest(T, f):
                ot = otmp[f % 2]
                bank = S.bank(ALLB7)
                pb = S.ps(bank)
                for j in range(4):
                    S.op("pe", lambda e, j=j: e.transpose(out=pb.ap[:, j * 128:(j + 1) * 128],
                                                          in_=ot.ap[:, j * 128:(j + 1) * 128], identity=identF.ap),
                         reads=[ot.all(), identF], writes=[pb], signal=(j == 3))
                od = outb.v(None, (f * 128, f * 128 + 128))
                S.op("dve", lambda e: e.tensor_tensor(out=od.ap, in0=od.ap,
                                                      in1=pb.ap.rearrange("p (j c) -> p j c", j=4), op=ALU.add),
                     reads=[pb, od], writes=[od])

            def store(T):
                orow = run * CH + 512 * T
                for j in range(4):
                    out_toks.append(S.dma("sp", out_d[orow + 128 * j:orow + 128 * j + 128, :], outb.ap[:, j, :], reads=[outb.v(j)]))

            xload(0)
            for f in range(16):
                first_f(0, f)
            first_end(0)
            stt(0, 0)
            for f in range(16):
                first_f(1, f)
                if f + 1 < 16:
                    stt(0, f + 1)
                rest(0, f)
            first_end(1)
            store(0)
            xload(1)
            stt(1, 0)
            for f in range(16):
                if f + 1 < 16:
                    stt(1, f + 1)
                rest(1, f)
            store(1)

        emit_run(0)
        emit_run(1)
        assert wstate["next"] == len(wseq)
        S.wait_tokens("sp", out_toks)
        with nc.Block() as block:
            S.replay(block)
    return nc


def _blocks(w, kc):
    K, N = w.shape
    nb = N // 128
    a = w.reshape(kc, 128, nb, 128)
    return np.ascontiguousarray(a.transpose(2, 1, 0, 3)).reshape(nb, 128, kc * 128)


def _masks(hv):
    k = np.arange(128)[:, None]
    q = np.arange(128)[None, :]
    LT = (k <= q).astype(np.float32)
    UT = (k >= q).astype(np.float32)
    lu4 = np.concatenate([LT, UT, LT, UT], axis=1)
    lu4h1 = np.concatenate([LT, UT * hv, LT, UT], axis=1)
    lu4h2 = np.concatenate([LT, UT * hv, LT, UT * hv], axis=1)
    g2a = np.concatenate([UT[:, 0:64] * hv] * 4 + [LT[:, 0:64]] * 4, axis=1)
    g2b = np.concatenate([LT[:, 64:128]] * 4 + [UT[:, 0:64] * hv] * 4, axis=1)
    return lu4, lu4h1, lu4h2, g2a, g2b


_PROG = {}


def kernel(x, c, positions, g_pre, w_ada, b_ada, w_in, conv_w, w_attn_o, w_conv_o, w_o, g_post):
    x = np.asarray(x, np.float32)[0]
    pos = np.asarray(positions, np.int32)[0]
    f32 = np.float32
    inv_freq = (500000.0 ** (-(np.arange(0, 32, 2, dtype=np.float64)) / 32.0)).astype(np.float32)
    try:
        import jax.numpy as jnp
        inv_freq = np.asarray(500000.0 ** (-jnp.arange(0, 32, 2, dtype=jnp.float32) / 32), np.float32)
    except Exception:
        pass
    win_r = _blocks(np.asarray(w_in, f32)[0], KC)
    wao_r = _blocks(np.asarray(w_attn_o, f32)[0], 8)
    wco_r = _blocks(np.asarray(w_conv_o, f32)[0], KC)
    wo_r = _blocks(np.asarray(w_o, f32)[0], KC)
    wada_r = np.ascontiguousarray(np.asarray(w_ada, f32)[0].reshape(KC, 128, 3 * D))

    def colT(v, n):
        return np.asarray(v, f32).reshape(n, 128).T

    Pm = np.zeros((128, 32), f32)
    for m in range(32):
        Pm[(m + 16) % 32, m] = 1.0
    in_maps = []
    for core in range(NCORES):
        hv = 0.0 if core == 0 else 1.0
        xin = np.zeros((HALO + TOK, D), f32)
        p32 = np.zeros((HALO + TOK,), np.int32)
        s = core * TOK
        if core > 0:
            xin[:HALO] = x[s - HALO:s]
            p32[:HALO] = pos[s - HALO:s]
        xin[HALO:] = x[s:s + TOK]
        p32[HALO:] = pos[s:s + TOK]
        cstf = np.zeros((128, NCF), f32)
        cstf[:, CF_IDENT:CF_IDENT + 128] = np.eye(128, dtype=f32)
        cstf[:, CF_GPRE:CF_GPRE + 16] = colT(g_pre[0], 16)
        cstf[:, CF_GPOST:CF_GPOST + 16] = colT(g_post[0], 16)
        cstf[:, CF_BADA:CF_BADA + 48] = colT(b_ada[0], 48)
        cstf[:, CF_C:CF_C + 16] = colT(c[0], 16)
        cw = np.asarray(conv_w, f32)[0]
        cstf[:, CF_CONVW:CF_CONVW + 48] = cw.T.reshape(16, 128, 3).transpose(1, 0, 2).reshape(128, 48)
        cstf[0:32, CF_INVF] = np.tile(inv_freq, 2)
        cstf[0:16, CF_SGN] = -1.0
        cstf[16:32, CF_SGN] = 1.0
        cstf[:, CF_ONE] = 1.0
        cstf[:, CF_HV] = hv
        cstf[:, CF_EPS] = EPS
        cstb = np.zeros((128, NCB), f32)
        cstb[:, CB_IDENT:CB_IDENT + 128] = np.eye(128, dtype=f32)
        cstb[:, CB_ONES:CB_ONES + 128] = 1.0
        cstb[:, CB_PM:CB_PM + 32] = Pm
        lu4, lu4h1, lu4h2, g2a, g2b = _masks(hv)
        cstb[:, CB_LU4:CB_LU4 + 512] = lu4
        cstb[:, CB_LU4H1:CB_LU4H1 + 512] = lu4h1
        cstb[:, CB_LU4H2:CB_LU4H2 + 512] = lu4h2
        cstb[:, CB_G2A:CB_G2A + 512] = g2a
        cstb[:, CB_G2B:CB_G2B + 512] = g2b
        in_maps.append({
            "xin": xin, "pos32": np.ascontiguousarray(np.broadcast_to(p32[None, :], (32, HALO + TOK))),
            "cstf": cstf, "cstb": cstb, "wada": wada_r, "win": win_r, "wao": wao_r, "wco": wco_r, "wo": wo_r,
        })
    if "nc" not in _PROG:
        _PROG["nc"] = build_program()
    res = run_bass_kernel_spmd(_PROG["nc"], in_maps, core_ids=list(range(NCORES)))
    out = np.concatenate([np.asarray(r["out"], f32) for r in res.results], axis=0)
    return out.reshape(1, SEQ, D)
```
